# Optimizing a Trainium2 kernel written in Bass

```python
import jax, jax.numpy as jnp
from jax import lax
import numpy as np

D_MODEL = 1024
BATCH = 16
SEQ = 2048
DEPTH = 2

MIX_WIDTH = D_MODEL
GROUP_WIDTH = MIX_WIDTH // 4
A_HEADS = 4
A_HEAD_DIM = GROUP_WIDTH // A_HEADS
A_CHUNK = 64
B_CONV_WIDTH = 31
C_CONV_WIDTH = 3
D_WINDOWS = (2, 4, 8, 16)
D_GROUPS = len(D_WINDOWS)
D_GROUP_DIM = GROUP_WIDTH // D_GROUPS
D_FF = ((8 * D_MODEL // 3 + 127) // 128) * 128
FFN_CONV_WIDTH = 3
A_COLS = 4 * GROUP_WIDTH
B_COLS = 2 * GROUP_WIDTH
C_COLS = 3 * GROUP_WIDTH
D_COLS = GROUP_WIDTH
IN_COLS = A_COLS + B_COLS + C_COLS + D_COLS
EPS = 1e-6
MIN_FORGET = 1e-30

kernel_name = "hybrid_parallel_hgrn2_conformer_shortconv_pool"


def rms_norm(x, g):
    xf = x.astype(jnp.float32)
    y = xf * lax.rsqrt(jnp.mean(xf * xf, axis=-1, keepdims=True) + EPS)
    return (y * g.astype(jnp.float32)).astype(x.dtype)


def layer_norm(x, g, b):
    xf = x.astype(jnp.float32)
    mu = jnp.mean(xf, axis=-1, keepdims=True)
    var = jnp.mean(jnp.square(xf - mu), axis=-1, keepdims=True)
    y = (xf - mu) * lax.rsqrt(var + EPS) * g.astype(jnp.float32) + b.astype(jnp.float32)
    return y.astype(x.dtype)


def causal_dwconv(x, w):
    width, ch = w.shape
    return lax.conv_general_dilated(
        x, w[:, None, :].astype(x.dtype), window_strides=(1,), padding=[(width - 1, 0)],
        dimension_numbers=("NWC", "WIO", "NWC"), feature_group_count=ch)


def hgrn2_chunked(q, k, v, log_f):
    bsz, t_len, h, kd = q.shape
    vd = v.shape[-1]
    n = t_len // A_CHUNK

    def to_chunks(a):
        return a.reshape(bsz, n, A_CHUNK, h, a.shape[-1]).transpose(1, 0, 3, 2, 4)

    qc, kc, vc, gc = to_chunks(q), to_chunks(k), to_chunks(v), to_chunks(log_f)
    mask = jnp.tril(jnp.ones((A_CHUNK, A_CHUNK), dtype=bool))[:, :, None]

    def step(state, inp):
        qi, ki, vi, gi = inp
        b = jnp.cumsum(gi, axis=-2)
        diff = b[..., :, None, :] - b[..., None, :, :]
        decay = jnp.where(mask, jnp.exp(jnp.where(mask, diff, 0.0)), 0.0)
        scores = jnp.einsum("bhtk,bhsk,bhtsk->bhts", qi, ki, decay)
        o_intra = jnp.einsum("bhts,bhsv->bhtv", scores, vi)
        o_inter = jnp.einsum("bhtk,bhkv->bhtv", qi * jnp.exp(b), state)
        b_last = b[..., -1:, :]
        new_state = jnp.exp(b_last)[..., 0, :, None] * state + jnp.einsum(
            "bhsk,bhsv->bhkv", ki * jnp.exp(b_last - b), vi)
        return new_state, o_intra + o_inter

    s0 = jnp.zeros((bsz, h, kd, vd), jnp.float32)
    _, o = lax.scan(step, s0, (qc, kc, vc, gc))
    return o.transpose(1, 0, 3, 2, 4).reshape(bsz, t_len, h, vd)


def hgrn2_mixer(p, lb, norm_g):
    bsz, t_len, _ = p.shape
    q_, f_, i_, g_ = jnp.split(p, 4, axis=-1)
    z = f_.astype(jnp.float32)
    lbf = lb.astype(jnp.float32)
    f = lbf + (1.0 - lbf) * jax.nn.sigmoid(z)
    log_f = jnp.log(jnp.maximum(f, MIN_FORGET))
    k = (1.0 - lbf) * jax.nn.sigmoid(-z)
    q = jax.nn.silu(q_.astype(jnp.float32)) * (A_HEAD_DIM ** -0.5)
    v = i_.astype(jnp.float32)
    heads = lambda a: a.reshape(bsz, t_len, A_HEADS, A_HEAD_DIM)
    o = hgrn2_chunked(heads(q), heads(k), heads(v), heads(log_f))
    o = o * lax.rsqrt(jnp.mean(o * o, axis=-1, keepdims=True) + EPS)
    o = o * norm_g.astype(jnp.float32).reshape(A_HEADS, A_HEAD_DIM)
    o = o.reshape(bsz, t_len, GROUP_WIDTH) * jax.nn.silu(g_.astype(jnp.float32))
    return o.astype(p.dtype)


def conformer_conv_mixer(p, dw_w, dw_b, ln_g, ln_b, pw_w, pw_b):
    a, gate = jnp.split(p, 2, axis=-1)
    h = a * jax.nn.sigmoid(gate)
    h = causal_dwconv(h, dw_w) + dw_b
    h = jax.nn.silu(layer_norm(h, ln_g, ln_b))
    return h @ pw_w + pw_b


def short_conv_mixer(p, conv_w):
    bg, cg, h = jnp.split(p, 3, axis=-1)
    return bg * causal_dwconv(cg * h, conv_w)


def pooling_mixer(u, proj, scale):
    bsz, t_len, ch = u.shape
    uf = u.astype(jnp.float32)
    cs = jnp.concatenate([jnp.zeros((bsz, 1, ch), jnp.float32), jnp.cumsum(uf, axis=1)], axis=1)
    pos = jnp.arange(t_len, dtype=jnp.float32)
    outs = []
    for g, w in enumerate(D_WINDOWS):
        sl = slice(g * D_GROUP_DIM, (g + 1) * D_GROUP_DIM)
        csg = cs[:, :, sl]
        lower = jnp.pad(csg[:, : t_len + 1 - w], ((0, 0), (w - 1, 0), (0, 0)))
        mean = (csg[:, 1:] - lower) / jnp.minimum(pos + 1.0, float(w))[None, :, None]
        outs.append(mean - uf[:, :, sl])
    pooled = jnp.stack(outs, axis=2)
    y = jnp.einsum("btgc,gcd->btgd", pooled, proj.astype(jnp.float32)).reshape(bsz, t_len, ch)
    return (y * scale.astype(jnp.float32)).astype(u.dtype)


def setup_inputs(seed: int = 0) -> dict:
    key = jax.random.key(seed)
    ks = jax.random.split(key, 21)
    f32 = jnp.float32
    nrm = lambda k, shape, s: jax.random.normal(k, shape, f32) * s
    gain = lambda k, shape: 1.0 + 0.1 * jax.random.normal(k, shape, f32)
    return {
        "x": nrm(ks[0], (BATCH, SEQ, D_MODEL), 1.0),
        "w_in": nrm(ks[1], (DEPTH, D_MODEL, IN_COLS), D_MODEL ** -0.5),
        "lb_gamma": nrm(ks[2], (DEPTH, GROUP_WIDTH), 0.5),
        "a_norm_g": gain(ks[3], (DEPTH, GROUP_WIDTH)),
        "b_dw_w": nrm(ks[4], (DEPTH, B_CONV_WIDTH, GROUP_WIDTH), B_CONV_WIDTH ** -0.5),
        "b_dw_b": nrm(ks[5], (DEPTH, GROUP_WIDTH), 0.02),
        "b_ln_g": gain(ks[6], (DEPTH, GROUP_WIDTH)),
        "b_ln_b": nrm(ks[7], (DEPTH, GROUP_WIDTH), 0.02),
        "b_pw_w": nrm(ks[8], (DEPTH, GROUP_WIDTH, GROUP_WIDTH), GROUP_WIDTH ** -0.5),
        "b_pw_b": nrm(ks[9], (DEPTH, GROUP_WIDTH), 0.02),
        "c_conv_w": nrm(ks[10], (DEPTH, C_CONV_WIDTH, GROUP_WIDTH), C_CONV_WIDTH ** -0.5),
        "d_proj": nrm(ks[11], (DEPTH, D_GROUPS, D_GROUP_DIM, D_GROUP_DIM), D_GROUP_DIM ** -0.5),
        "d_scale": gain(ks[12], (DEPTH, GROUP_WIDTH)),
        "w_out": nrm(ks[13], (DEPTH, MIX_WIDTH, D_MODEL), MIX_WIDTH ** -0.5),
        "mix_pre_g": gain(ks[14], (DEPTH, D_MODEL)),
        "mix_post_g": gain(ks[15], (DEPTH, D_MODEL)),
        "ffn_pre_g": gain(ks[16], (DEPTH, D_MODEL)),
        "ffn_post_g": gain(ks[17], (DEPTH, D_MODEL)),
        "w_up": nrm(ks[18], (DEPTH, D_MODEL, 2 * D_FF), D_MODEL ** -0.5),
        "ffn_conv_w": nrm(ks[19], (DEPTH, FFN_CONV_WIDTH, 2 * D_FF), FFN_CONV_WIDTH ** -0.5),
        "w_down": nrm(ks[20], (DEPTH, D_FF, D_MODEL), D_FF ** -0.5),
    }


def reference(x, w_in, lb_gamma, a_norm_g, b_dw_w, b_dw_b, b_ln_g, b_ln_b, b_pw_w, b_pw_b,
              c_conv_w, d_proj, d_scale, w_out, mix_pre_g, mix_post_g, ffn_pre_g, ffn_post_g,
              w_up, ffn_conv_w, w_down):
    lb_soft = jax.nn.softmax(lb_gamma.astype(jnp.float32), axis=0)
    lower_bounds = jnp.cumsum(lb_soft, axis=0) - lb_soft[0:1]
    o_a, o_b, o_c = A_COLS, A_COLS + B_COLS, A_COLS + B_COLS + C_COLS
    for l in range(DEPTH):
        h = rms_norm(x, mix_pre_g[l])
        p = h @ w_in[l]
        y_a = hgrn2_mixer(p[..., :o_a], lower_bounds[l], a_norm_g[l])
        y_b = conformer_conv_mixer(p[..., o_a:o_b], b_dw_w[l], b_dw_b[l], b_ln_g[l], b_ln_b[l],
                                   b_pw_w[l], b_pw_b[l])
        y_c = short_conv_mixer(p[..., o_b:o_c], c_conv_w[l])
        y_d = pooling_mixer(p[..., o_c:], d_proj[l], d_scale[l])
        y = jnp.concatenate([y_a, y_b, y_c, y_d], axis=-1) @ w_out[l]
        x = x + rms_norm(y, mix_post_g[l])
        h = rms_norm(x, ffn_pre_g[l])
        u = causal_dwconv(h @ w_up[l], ffn_conv_w[l])
        gate, up = jnp.split(u, 2, axis=-1)
        y = (jax.nn.silu(gate) * up) @ w_down[l]
        x = x + rms_norm(y, ffn_post_g[l])
    return x
```

```python
import math
from contextlib import ExitStack

import numpy as np
import concourse.bass as bass
import concourse.mybir as mybir
from concourse.bass_utils import run_bass_kernel_spmd

F32 = mybir.dt.float32
BF16 = mybir.dt.bfloat16
ALU = mybir.AluOpType
AF = mybir.ActivationFunctionType
AX = mybir.AxisListType

NCORES = 8
D = 1024
SEQ = 2048
BATCH = 16
DEPTH = 2
DFF = 2816
NT = 512
ST = 1024
EPS = 1e-6
LN_MINF = math.log(1e-30)

C_PRE, C_POST, C_FPRE, C_FPOST = 0, 8, 16, 24
C_ANG, C_DWB, C_LNG, C_LNB, C_PWB, C_DSC, C_LBG = 32, 34, 36, 38, 40, 42, 44
C_DWW = 46
C_CCW = 108
C_FCW = 114
NCOL = C_FCW + 44 * 3


class Buf:
    __slots__ = ("w", "r", "name")

    def __init__(self, name=""):
        self.w = None
        self.r = {}
        self.name = name


class V:
    __slots__ = ("ap", "bufs", "tok")

    def __init__(self, ap, bufs, tok=None):
        self.ap = ap
        self.bufs = bufs
        self.tok = tok


class BankRef:
    def __init__(self, bank):
        self.bank = bank
        self.t = bank.t
        self.b = bank.b
        self.token = bank.token

    def v(self, idx=slice(None), slot=0):
        return V(self.t[idx], [self.b[0]], (self.bank, self.token))

    def w(self, ap):
        return V(ap, [self.b[0]], (self.bank, self.token))


class T:
    def __init__(self, t, nslots=1, name=""):
        self.t = t
        self.b = [Buf(f"{name}{i}") for i in range(nslots)]

    def v(self, idx, slot=0):
        return V(self.t[idx], [self.b[slot]])

    def va(self, idx):
        return V(self.t[idx], list(self.b))


class Slot:
    def __init__(self, sem):
        self.sem = sem
        self.cnt = 0


class Sched:
    EPOCH = 30000

    def __init__(self, nc, es):
        self.nc = nc
        self.es = es
        self.E = {"pe": nc.tensor, "act": nc.scalar, "dve": nc.vector, "pool": nc.gpsimd, "sp": nc.sync}
        self.sem = {}
        self.cnt = {}
        self.nsem = 0
        self.waited = {e: {} for e in self.E}
        self.last = {}
        self.slots = []
        self.nwait = 0
        self.nins = 0

    def _newsem(self, name):
        self.nsem += 1
        return self.es.enter_context(self.nc.semaphore(f"{name}{self.nsem}"))

    def slot(self, name):
        s = Slot(self._newsem("d" + name))
        self.slots.append(s)
        return s

    def _ticket(self, e):
        if e not in self.sem or self.cnt[e] >= self.EPOCH:
            self.sem[e] = self._newsem("e" + e)
            self.cnt[e] = 0
        self.cnt[e] += 1
        t = (self.sem[e], self.cnt[e], e)
        self.last[e] = t
        return t

    def _deps(self, e, reads, writes):
        deps = {}

        def add(t):
            if t is None:
                return
            k = id(t[0])
            if k not in deps or deps[k][1] < t[1]:
                deps[k] = t

        for v in reads:
            for b in v.bufs:
                if b.w is not None and not (b.w[2] == e and e == "pe"):
                    add(b.w)
        for v in writes:
            for b in v.bufs:
                if b.w is not None and b.w[2] != e:
                    add(b.w)
                for k, t in b.r.items():
                    if t[2] != e:
                        add(t)
        return deps

    def _emit_waits(self, e, deps):
        eng = self.E[e]
        w = self.waited[e]
        for k, t in deps.items():
            if w.get(k, 0) >= t[1]:
                continue
            eng.wait_ge(t[0], t[1])
            w[k] = t[1]
            self.nwait += 1

    def op(self, e, fn, reads=(), writes=()):
        for v in list(reads) + list(writes):
            if v.tok is not None and v.tok[0].token is not v.tok[1]:
                raise RuntimeError("stale PSUM bank view (bank re-allocated before this use was issued)")
        self._emit_waits(e, self._deps(e, reads, writes))
        ins = fn()
        t = self._ticket(e)
        ins.then_inc(t[0], 1)
        self.nins += 1
        for v in writes:
            for b in v.bufs:
                b.w = t
                b.r = {}
        for v in reads:
            for b in v.bufs:
                b.r[e] = t
        return t

    def dma(self, e, out, in_, slot, out_v=None, in_v=None):
        reads = [in_v] if in_v is not None else []
        writes = [out_v] if out_v is not None else []
        deps = {}
        for v in reads:
            for b in v.bufs:
                if b.w is not None:
                    deps[id(b.w[0])] = b.w
        for v in writes:
            for b in v.bufs:
                if b.w is not None and b.w[0] is not slot.sem:
                    deps[id(b.w[0])] = b.w
                for k, t in b.r.items():
                    kk = id(t[0])
                    if kk not in deps or deps[kk][1] < t[1]:
                        deps[kk] = t
        self._emit_waits(e, deps)
        self.E[e].dma_start(out=out, in_=in_).then_inc(slot.sem, 16)
        slot.cnt += 16
        t = (slot.sem, slot.cnt, "dma")
        for v in writes:
            for b in v.bufs:
                b.w = t
                b.r = {}
        for v in reads:
            for b in v.bufs:
                b.r["dma" + str(id(slot))] = t
        return t

    def barrier(self, engines=("pe", "act", "dve")):
        ts = [t for k, t in self.last.items() if k in ("pe", "act", "dve", "pool")]
        for e in engines:
            deps = {}
            for t in ts:
                if t[2] == e:
                    continue
                deps[id(t[0])] = t
            self._emit_waits(e, deps)


def build_program(n_layers=DEPTH, n_st=4, do_ffn=True, stage=99, debug=False):
    nc = bass.Bass("TRN2", target_bir_lowering=False)
    dr = lambda name, shape: nc.dram_tensor(name, shape, F32, kind="ExternalInput").ap()
    x_d = dr("x_fm", [2, D, SEQ])
    w_in_d = dr("w_in_r", [DEPTH, 128, 8, 2560])
    w_out_d = dr("w_out_r", [DEPTH, 128, 8, D])
    w_up_d = dr("w_up_r", [DEPTH, 128, 8, 2 * DFF])
    w_dn_d = dr("w_down_r", [DEPTH, 128, 22, D])
    bpw_d = dr("b_pw_r", [DEPTH, 128, 2, 256])
    dpj_d = dr("d_proj_r", [DEPTH, 128, 2, 128])
    cols_d = dr("cols", [DEPTH, 128, NCOL])
    ident_d = dr("c_ident", [128, 128])
    maskT_d = dr("c_maskT", [64, 512])
    bmask_d = dr("c_bmask", [128, 128])
    rmask_d = dr("c_rmask", [128, 512])
    dtap_d = dr("c_dtap", [128, 20, 128])
    dcorr_d = dr("c_dcorr", [128, 2, 16])
    hm8_d = dr("c_hm8", [128, 2])
    out_d = nc.dram_tensor("out_fm", [2, D, SEQ], F32, kind="ExternalOutput").ap()
    dbg_d = nc.dram_tensor("dbg", [128, 8, ST], BF16, kind="ExternalOutput").ap() if debug else None

    with ExitStack() as es:
        S = Sched(nc, es)
        E = S.E

        uid = [0]

        def sb(name, shape, dt, nslots=1, scope=es):
            uid[0] += 1
            return T(scope.enter_context(nc.sbuf_tensor(f"{name}_{uid[0]}", shape, dt)), nslots, name)

        banks = [T(es.enter_context(nc.psum_tensor(f"ps{i}", [128, 512], F32)), 1, f"ps{i}") for i in range(8)]
        rotation = list(banks)

        def psum():
            b = rotation.pop(0)
            rotation.append(b)
            b.token = object()
            return BankRef(b)

        def psum_hold():
            b = rotation.pop(0)
            b.token = object()
            return BankRef(b)

        def psum_release(ref):
            rotation.append(ref.bank)

        X = sb("X", [128, 8, ST], F32, 16)
        H = sb("H", [128, 8, ST], BF16, 2)
        COLS = sb("COLS", [128, DEPTH, NCOL], F32)
        LB = sb("LB", [128, 3, DEPTH, 2], F32)
        IDB = sb("IDB", [128, 128], BF16)
        ONEB = sb("ONEB", [128, 128], BF16)
        ONEF = sb("ONEF", [128, 128], F32)
        MASKT = sb("MASKT", [64, 512], F32)
        BMASK = sb("BMASK", [128, 128], F32)
        RMASK = sb("RMASK", [128, 512], F32)
        DTAP = sb("DTAP", [128, 20, 128], BF16)
        DCORR = sb("DCORR", [128, 2, 16], F32)
        HM8 = sb("HM8", [128, 2], F32)
        HBh = [sb(f"HBh{l}", [128, 2, 30], BF16) for l in range(DEPTH)]
        MCh = [sb(f"MCh{l}", [128, 2, 2], BF16) for l in range(DEPTH)]
        UDh = [sb(f"UDh{l}", [128, 2, 15], BF16) for l in range(DEPTH)]
        FH = [sb(f"FH{l}", [128, 44, 2], BF16) for l in range(DEPTH)]
        S32 = [sb(f"S32_{l}", [128, 2, 128], F32, 2) for l in range(DEPTH)]
        SBF = [sb(f"SBF{l}", [128, 2, 2, 128], BF16, 4) for l in range(DEPTH)]
        WS0 = sb("WS0", [128, 8, 1024], BF16)
        WS1 = sb("WS1", [128, 8, 768], BF16, 2)
        WSM = sb("WSM", [128, 4, 256], BF16, 2)

        sl_c = S.slot("c")
        sl_x = S.slot("x")
        sl_o = S.slot("o")
        sl_w0 = S.slot("w0")
        sl_w1 = S.slot("w1")
        sl_wm = S.slot("wm")

        def act(out, in_, func, reads=None, scale=1.0, bias=0.0, extra_reads=()):
            rd = [in_] + list(extra_reads)
            kw = {}
            if isinstance(scale, V):
                rd.append(scale)
                kw["scale"] = scale.ap
            else:
                kw["scale"] = float(scale)
            if isinstance(bias, V):
                rd.append(bias)
                kw["bias"] = bias.ap
            elif bias != 0.0:
                kw["bias"] = float(bias)
            return S.op("act", lambda: E["act"].activation(out=out.ap, in_=in_.ap, func=func, **kw), rd, [out])

        def tt(e, out, a, b, op):
            return S.op(e, lambda: E[e].tensor_tensor(out=out.ap, in0=a.ap, in1=b.ap, op=op), [a, b], [out])

        def ts(e, out, a, s1, s2, op0, op1=None):
            rd = [a]
            k1 = s1.ap if isinstance(s1, V) else float(s1)
            if isinstance(s1, V):
                rd.append(s1)
            if s2 is None:
                return S.op(e, lambda: E[e].tensor_scalar(out=out.ap, in0=a.ap, scalar1=k1, scalar2=0.0, op0=op0, op1=ALU.add), rd, [out])
            k2 = s2.ap if isinstance(s2, V) else float(s2)
            if isinstance(s2, V):
                rd.append(s2)
            return S.op(e, lambda: E[e].tensor_scalar(out=out.ap, in0=a.ap, scalar1=k1, scalar2=k2, op0=op0, op1=op1), rd, [out])

        def stt(e, out, a, s, b, op0, op1):
            rd = [a, b]
            k = s.ap if isinstance(s, V) else float(s)
            if isinstance(s, V):
                rd.append(s)
            return S.op(e, lambda: E[e].scalar_tensor_tensor(out=out.ap, in0=a.ap, scalar=k, in1=b.ap, op0=op0, op1=op1), rd, [out])

        def copy(e, out, in_):
            if e == "act":
                return act(out, in_, AF.Copy)
            return S.op(e, lambda: E[e].tensor_copy(out=out.ap, in_=in_.ap), [in_], [out])

        def memset(e, out, val):
            return S.op(e, lambda: E[e].memset(out.ap, val), [], [out])

        def mm(out, pairs):
            rd = []
            for l, r in pairs:
                rd += [l, r]
            n = len(pairs)

            def fn():
                ins = None
                for i, (l, r) in enumerate(pairs):
                    ins = E["pe"].matmul(out.ap, lhsT=l.ap, rhs=r.ap, start=(i == 0), stop=(i == n - 1))
                return ins

            return S.op("pe", fn, rd, [out])

        def col(l, c):
            return COLS.v((slice(None), l, slice(c, c + 1)))

        A_ = slice(None)

        S.dma("sp", COLS.t[:, :, :], cols_d.rearrange("l p n -> p l n"), sl_c, out_v=COLS.v(A_))
        S.dma("pool", IDB.t[:, :], ident_d[:, :], sl_c, out_v=IDB.v(A_))
        S.dma("sp", MASKT.t[:, :], maskT_d[:, :], sl_c, out_v=MASKT.v(A_))
        S.dma("sp", BMASK.t[:, :], bmask_d[:, :], sl_c, out_v=BMASK.v(A_))
        S.dma("sp", RMASK.t[:, :], rmask_d[:, :], sl_c, out_v=RMASK.v(A_))
        S.dma("pool", DTAP.t[:, :, :], dtap_d[:, :, :], sl_c, out_v=DTAP.v(A_))
        S.dma("sp", DCORR.t[:, :, :], dcorr_d[:, :, :], sl_c, out_v=DCORR.v(A_))
        S.dma("sp", HM8.t[:, :], hm8_d[:, :], sl_c, out_v=HM8.v(A_))
        memset("dve", ONEB.v(A_), 1.0)
        memset("dve", ONEF.v(A_), 1.0)
        with ExitStack() as ps:
            EX = sb("lbEX", [128, DEPTH, 2], F32, scope=ps)
            SM = sb("lbSM", [128, 2], F32, scope=ps)
            LS = sb("lbLS", [128, DEPTH, 2], F32, scope=ps)
            CS = sb("lbCS", [128, 2], F32, scope=ps)
            for l in range(DEPTH):
                act(EX.v((A_, l, A_)), COLS.v((A_, l, slice(C_LBG, C_LBG + 2))), AF.Exp)
            tt("dve", SM.v(A_), EX.v((A_, 0, A_)), EX.v((A_, 1, A_)), ALU.add)
            S.op("dve", lambda: E["dve"].reciprocal(out=SM.t[:, :], in_=SM.t[:, :]), [SM.v(A_)], [SM.v(A_)])
            for l in range(DEPTH):
                tt("dve", LS.v((A_, l, A_)), EX.v((A_, l, A_)), SM.v(A_), ALU.mult)
            copy("dve", CS.v(A_), LS.v((A_, 0, A_)))
            tt("dve", LB.v((A_, 0, 0, A_)), CS.v(A_), LS.v((A_, 0, A_)), ALU.subtract)
            for l in range(1, DEPTH):
                tt("dve", CS.v(A_), CS.v(A_), LS.v((A_, l, A_)), ALU.add)
                tt("dve", LB.v((A_, 0, l, A_)), CS.v(A_), LS.v((A_, 0, A_)), ALU.subtract)
            ts("dve", LB.v((A_, 1, A_, A_)), LB.v((A_, 0, A_, A_)), -1.0, 1.0, ALU.mult, ALU.add)
            ts("dve", LB.v((A_, 2, A_, A_)), LB.v((A_, 1, A_, A_)), -1.0, None, ALU.mult)
            S.barrier()

        def lbv(kind, l, c):
            return LB.v((A_, kind, l, slice(c, c + 1)))

        def rstd_from_ss(RSTD, ss_bank, n):
            act(RSTD, ss_bank, AF.Ln, scale=1.0 / n, bias=EPS_T)
            act(RSTD, RSTD, AF.Exp, scale=-0.5)

        EPSC = sb("EPSC", [128, 1], F32)
        memset("dve", EPSC.v(A_), EPS)
        EPS_T = EPSC.v(A_)

        PSQ = sb("PSQ", [128, 2, 512], BF16, 2)
        PRS = sb("PRS", [128, 512], F32, 1)

        def prenorm_tile(l, gcol, tt_):
            cs = slice(tt_ * NT, (tt_ + 1) * NT)
            ssb = psum_hold()
            for k in range(8):
                sq = PSQ.v((A_, k % 2, A_), k % 2)
                act(sq, X.v((A_, k, cs), k * 2 + tt_), AF.Square)
                mm_acc(ssb.v(A_), ONEB.v(A_), sq, k == 0, k == 7)
            rstd_from_ss(PRS.v(A_), ssb.v(A_), float(D))
            psum_release(ssb)
            for k in range(8):
                stt("dve", H.v((A_, k, cs), tt_), X.v((A_, k, cs), k * 2 + tt_), col(l, gcol + k), PRS.v(A_), ALU.mult, ALU.mult)

        def prenorm(l, gcol, scope=None):
            for tt_ in range(ST // NT):
                prenorm_tile(l, gcol, tt_)

        def mm_acc(out, l, r, start, stop):
            return S.op("pe", lambda: E["pe"].matmul(out.ap, lhsT=l.ap, rhs=r.ap, start=start, stop=stop), [l, r], [out])

        def proj(W, c0, tt_, wslot=None):
            cs = slice(tt_ * NT, (tt_ + 1) * NT)
            b = psum()
            wv = (lambda i: W.va(i)) if wslot is None else (lambda i: W.v(i, wslot))
            mm(b.v(A_), [(wv((A_, k, slice(c0, c0 + 128))), H.v((A_, k, cs), tt_)) for k in range(8)])
            return b

        def postnorm_residual(l, gcol, YT, SSB, tt_):
            cs = slice(tt_ * NT, (tt_ + 1) * NT)
            RS = YT["RS"]
            rstd_from_ss(RS.v(A_), SSB.v(A_), float(D))
            for m in range(8):
                y = YT["Y"].v((A_, m, A_), m)
                tt("dve", y, y, RS.v(A_), ALU.mult)
                stt("dve", X.v((A_, m, cs), m * 2 + tt_), y, col(l, gcol + m), X.v((A_, m, cs), m * 2 + tt_), ALU.mult, ALU.add)

        n_tiles = ST // NT
        for st in range(n_st):
            seq, half = st // 2, st % 2
            first = half == 0
            t0 = half * ST
            for xt_ in range(ST // NT):
                xs_ = slice(xt_ * NT, (xt_ + 1) * NT)
                S.dma("sp", X.t[:, :, xs_], x_d[seq, :, t0 + xt_ * NT:t0 + (xt_ + 1) * NT].rearrange("(k p) t -> p k t", p=128), sl_x,
                      out_v=V(X.t[:, :, xs_], [X.b[k * 2 + xt_] for k in range(8)]))
            for l in range(n_layers):
                with ExitStack() as mx:
                    YC = sb("YC", [128, 8, ST], BF16, 16, scope=mx)
                    if stage >= 2 and (l == 0 or not do_ffn):
                        prenorm(l, C_PRE, mx)
                    S.dma("pool", WS0.t[:, :, 0:1024], w_in_d[l, :, :, 0:1024], sl_w0, out_v=WS0.v(A_))
                    with ExitStack() as sa:
                      if stage >= 3:
                        TMP = sb("aTMP", [128, 7, 512], F32, 7, scope=sa)
                        QT = sb("aQT", [128, 2, 2, 2, 512], BF16, 8, scope=sa)
                        KT = sb("aKT", [128, 2, 2, 512], BF16, 4, scope=sa)
                        VF = sb("aVF", [128, 2, 2, 512], BF16, 4, scope=sa)
                        SG = sb("aSG", [128, 2, 2, 512], F32, 4, scope=sa)
                        DL = sb("aDL", [128, 2, 2, 8], F32, 4, scope=sa)
                        KV = sb("aKV", [64, 2, 8, 512], BF16, 16, scope=sa)
                        PT = sb("aPT", [64, 2, 4, 512], BF16, 8, scope=sa)
                        DB = sb("aDB", [128, 8, 2, 128], F32, 8, scope=sa)
                        ON = sb("aON", [64, 4, 512], BF16, 4, scope=sa)
                        OSQ = sb("aOSQ", [64, 4, 512], F32, 4, scope=sa)
                        MS = sb("aMS", [64, 4, 8], F32, 4, scope=sa)
                        STMP = sb("aSTMP", [128, 2, 128], F32, 1, scope=sa)
                        SBS = sb("aSBS", [128, 2, 9, 2, 128], BF16, 18, scope=sa)
                        if first:
                            for c in range(2):
                                memset("dve", S32[l].v((A_, c, A_), c), 0.0)
                                memset("dve", SBF[l].v((A_, 0, c, A_), c), 0.0)
                        ast = {}

                        def A1(t_):
                            QSs = [TMP.v((A_, i, A_), i) for i in (0, 1)]
                            SIGs = [TMP.v((A_, i, A_), i) for i in (2, 3)]
                            LOGF, KK, BC = [TMP.v((A_, i, A_), i) for i in (4, 5, 6)]
                            EB = LOGF
                            for c in range(2):
                                b = proj(WS0, c * 128, t_)
                                act(QSs[c], b.v(A_), AF.Silu)
                            for c in range(2):
                                b = proj(WS0, 768 + c * 128, t_)
                                act(SG.v((A_, t_, c, A_), t_ * 2 + c), b.v(A_), AF.Silu)
                            for c in range(2):
                                b = proj(WS0, 256 + c * 128, t_)
                                act(SIGs[c], b.v(A_), AF.Sigmoid)
                            for c in range(2):
                                b = proj(WS0, 512 + c * 128, t_)
                                copy("act", VF.v((A_, t_, c, A_), t_ * 2 + c), b.v(A_))
                            for c in range(2):
                                QS, SIG = QSs[c], SIGs[c]
                                ENB = SIG
                                act(LOGF, SIG, AF.Ln, scale=lbv(1, l, c), bias=lbv(0, l, c))
                                ts("dve", LOGF, LOGF, LN_MINF, None, ALU.max)
                                ts("dve", KK, SIG, lbv(2, l, c), lbv(1, l, c), ALU.mult, ALU.add)
                                S.op("dve", lambda: E["dve"].tensor_tensor_scan(
                                    out=BC.ap, data0=RMASK.t[:, :], data1=LOGF.ap, initial=0.0, op0=ALU.mult, op1=ALU.add),
                                    [RMASK.v(A_), LOGF], [BC])
                                act(EB, BC, AF.Exp)
                                act(ENB, BC, AF.Exp, scale=-1.0)
                                for hh in range(2):
                                    stt("dve", QT.v((A_, t_, c, hh, A_), t_ * 4 + c * 2 + hh), QS, HM8.v((A_, slice(hh, hh + 1))), EB, ALU.mult, ALU.mult)
                                tt("dve", KT.v((A_, t_, c, A_), t_ * 2 + c), KK, ENB, ALU.mult)
                                copy("dve", DL.v((A_, t_, c, A_), t_ * 2 + c), V(TMP.t[:, 4, 63:512:64], [TMP.b[4]]))

                        def A2(t_):
                            for j in range(8):
                                js = slice(j * 64, (j + 1) * 64)
                                b = psum()
                                srcs = [KT.v((A_, t_, 0, js), t_ * 2), KT.v((A_, t_, 1, js), t_ * 2 + 1),
                                        VF.v((A_, t_, 0, js), t_ * 2), VF.v((A_, t_, 1, js), t_ * 2 + 1)]

                                def fn(b=b, srcs=srcs):
                                    ins = None
                                    for i, s_ in enumerate(srcs):
                                        ins = E["pe"].matmul(b.t[0:64, i * 128:(i + 1) * 128], lhsT=s_.ap, rhs=IDB.t[:, :], start=True, stop=True)
                                    return ins

                                S.op("pe", fn, srcs + [IDB.v(A_)], [b.v(A_)])
                                copy("act", KV.v((slice(0, 64), t_, j, A_), t_ * 8 + j), b.w(b.t[0:64, :]))
                            spb = []
                            for jp in range(4):
                                b = psum()

                                def fn(b=b, jp=jp):
                                    ins = None
                                    for jj in range(2):
                                        j = jp * 2 + jj
                                        for c in range(2):
                                            ins = E["pe"].matmul(b.t[:, (jj * 2 + c) * 128:(jj * 2 + c + 1) * 128],
                                                                 lhsT=KV.t[0:64, t_, j, c * 128:(c + 1) * 128],
                                                                 rhs=KV.t[0:64, t_, j, 256 + c * 128: 256 + (c + 1) * 128], start=True, stop=True)
                                    return ins

                                S.op("pe", fn, [KV.v(A_, t_ * 8 + jp * 2), KV.v(A_, t_ * 8 + jp * 2 + 1)], [b.v(A_)])
                                spb.append(b)
                            ast[t_] = spb

                        def A3(t_):
                            for jp in range(4):
                                b = psum()
                                rd = [KT.v(A_, t_ * 2), KT.v(A_, t_ * 2 + 1)] + [QT.v(A_, t_ * 4 + i) for i in range(4)]

                                def fn(b=b, jp=jp):
                                    ins = None
                                    for jj in range(2):
                                        j = jp * 2 + jj
                                        js = slice(j * 64, (j + 1) * 64)
                                        for h in range(4):
                                            c, hh = h // 2, h % 2
                                            ins = E["pe"].matmul(b.t[0:64, jj * 256 + h * 64: jj * 256 + (h + 1) * 64],
                                                                 lhsT=KT.t[:, t_, c, js], rhs=QT.t[:, t_, c, hh, js],
                                                                 start=True, stop=True)
                                    return ins

                                S.op("pe", fn, rd, [b.v(A_)])
                                tt("dve", PT.v((slice(0, 64), t_, jp, A_), t_ * 4 + jp), b.w(b.t[0:64, :]), MASKT.v(A_), ALU.mult)

                        def A4(t_):
                            spb = ast[t_]
                            for j in range(8):
                                for c in range(2):
                                    dlv = DL.v((A_, t_, c, slice(j, j + 1)), t_ * 2 + c)
                                    S.op("dve", lambda j=j, c=c, dlv=dlv: E["dve"].tensor_scalar_mul(out=DB.t[:, j, c, :], in0=BMASK.t[:, :], scalar1=dlv.ap),
                                         [BMASK.v(A_), dlv], [DB.v(A_, j)])
                            s32 = S32[l].va(A_)
                            copy("act", SBS.v((A_, t_, 0, A_, A_), t_ * 9), SBF[l].va((A_, 0, A_, A_)))
                            for j in range(8):
                                jp, jj = j // 2, j % 2
                                spv = spb[jp].w(spb[jp].t[:, jj * 256:(jj + 1) * 256].rearrange("p (c v) -> p c v", c=2))
                                tt("dve", STMP.v(A_), spv, s32, ALU.add)
                                tt("dve", s32, STMP.v(A_), DB.v((A_, j, A_, A_), j), ALU.mult)
                                copy("act", SBS.v((A_, t_, j + 1, A_, A_), t_ * 9 + j + 1), s32)
                            copy("act", SBF[l].va((A_, 0, A_, A_)), s32)

                        def A5(t_):
                            obs = []
                            for jp in range(4):
                                ob = psum()
                                for jj in range(2):
                                    j = jp * 2 + jj
                                    js = slice(j * 64, (j + 1) * 64)

                                    def fn(ob=ob, jp=jp, jj=jj, j=j, js=js):
                                        ins = None
                                        for h in range(4):
                                            c, hp = h // 2, (h % 2) * 64
                                            o_ap = ob.t[0:64, jj * 256 + h * 64: jj * 256 + (h + 1) * 64]
                                            E["pe"].matmul(o_ap, lhsT=PT.t[0:64, t_, jp, jj * 256 + h * 64: jj * 256 + (h + 1) * 64],
                                                           rhs=KV.t[0:64, t_, j, 256 + h * 64: 256 + (h + 1) * 64], start=True, stop=False)
                                            ins = E["pe"].matmul(o_ap, lhsT=QT.t[:, t_, c, h % 2, js],
                                                                 rhs=SBS.t[:, t_, j, c, hp:hp + 64], start=False, stop=True)
                                        return ins

                                    S.op("pe", fn, [PT.v(A_, t_ * 4 + jp), KV.v(A_, t_ * 8 + j)] + [QT.v(A_, t_ * 4 + i) for i in range(4)]
                                         + [SBS.v(A_, t_ * 9 + j)], [ob.v(A_)])
                                obs.append(ob)
                            for jp in range(4):
                                act(OSQ.v((A_, jp, A_), jp), obs[jp].w(obs[jp].t[0:64, :]), AF.Square)
                            for jp in range(4):
                                S.op("dve", lambda jp=jp: E["dve"].tensor_reduce(
                                    out=MS.t[:, jp, :], in_=OSQ.t[:, jp, :].rearrange("p (g v) -> p g v", v=64), axis=AX.X, op=ALU.add),
                                    [OSQ.v(A_, jp)], [MS.v(A_, jp)])
                            for jp in range(4):
                                msv = MS.v((A_, jp, A_), jp)
                                act(msv, msv, AF.Ln, scale=1.0 / 64.0, bias=V(EPSC.t[0:64, :], EPSC.b))
                                act(msv, msv, AF.Exp, scale=-0.5)
                            for jp in range(4):
                                ob = obs[jp]
                                S.op("dve", lambda ob=ob, jp=jp: E["dve"].tensor_tensor(
                                    out=ON.t[0:64, jp, :].rearrange("p (g v) -> p g v", v=64),
                                    in0=ob.t[0:64, :].rearrange("p (g v) -> p g v", v=64),
                                    in1=MS.t[:, jp, :].unsqueeze(2).to_broadcast([64, 8, 64]), op=ALU.mult),
                                    [ob.w(ob.t[0:64, :]), MS.v(A_, jp)], [ON.v(A_, jp)])

                        def A6(t_):
                            cs = slice(t_ * NT, (t_ + 1) * NT)
                            for c in range(2):
                                yb = psum()

                                def fn(yb=yb, c=c):
                                    ins = None
                                    for j in range(8):
                                        jp, jj = j // 2, j % 2
                                        ins = E["pe"].matmul(yb.t[:, j * 64:(j + 1) * 64],
                                                             lhsT=ON.t[0:64, jp, jj * 256 + c * 128: jj * 256 + (c + 1) * 128],
                                                             rhs=IDB.t[0:64, 0:64], start=True, stop=True)
                                    return ins

                                S.op("pe", fn, [ON.va(A_), IDB.v(A_)], [yb.v(A_)])
                                stt("dve", YC.v((A_, c, cs), c * 2 + t_), yb.v(A_), col(l, C_ANG + c), SG.v((A_, t_, c, A_), t_ * 2 + c), ALU.mult, ALU.mult)

                        A1(0)
                        A1(1)
                        if stage >= 3.2:
                            A2(0)
                            A3(0)
                            A4(0)
                            A2(1)
                            A3(1)
                            A4(1)
                            A5(0)
                            A6(0)
                            A5(1)
                            A6(1)
                        S.barrier()
                    S.dma("pool", WS1.t[:, :, 0:512], w_in_d[l, :, :, 1024:1536], sl_w1, out_v=WS1.v(A_, 0))
                    S.dma("pool", WSM.t[:, 0:2, :], bpw_d[l, :, :, :], sl_wm, out_v=WSM.v(A_, 0))
                    S.dma("pool", WS0.t[:, :, 0:768], w_in_d[l, :, :, 1536:2304], sl_w0, out_v=WS0.v(A_))
                    S.dma("pool", WS1.t[:, :, 512:768], w_in_d[l, :, :, 2304:2560], sl_w1, out_v=WS1.v(A_, 1))
                    S.dma("pool", WSM.t[:, 2:4, 0:128], dpj_d[l, :, :, :], sl_wm, out_v=WSM.v(A_, 1))
                    MCF = sb("cMCF", [128, 2, 2 + ST], BF16, 2, scope=mx)
                    DGC = sb("cDG", [128, 6, 128], BF16, 1, scope=mx)
                    HC = sb("cHC", [128, 2, 512], F32, 2, scope=mx)
                    BG = sb("cBG", [128, 2, 2, 512], F32, 4, scope=mx)
                    UDF = sb("dUDF", [128, 2, 15 + ST], BF16, 2, scope=mx)
                    UF = sb("dUF", [128, 2, 2, 512], F32, 4, scope=mx)
                    PB = sb("dPB", [128, 2, 2, 512], BF16, 4, scope=mx)
                    T16 = sb("dT16", [128, 2, 16], F32, 2, scope=mx)
                    with ExitStack() as sbk:
                      if stage >= 4:
                        HBF = sb("bHBF", [128, 2, 30 + ST], BF16, 2, scope=sbk)
                        DG = sb("bDG", [128, 62, 128], BF16, 2, scope=sbk)
                        SGB = sb("bSGB", [128, 2, 512], F32, 2, scope=sbk)
                        CB = sb("bCB", [128, 2, 2, 512], F32, 4, scope=sbk)
                        SQB = sb("bSQB", [128, 2, 2, 512], F32, 4, scope=sbk)
                        MU = sb("bMU", [128, 2, 512], F32, 2, scope=sbk)
                        RB = sb("bRB", [128, 2, 512], F32, 2, scope=sbk)
                        T1 = sb("bT1", [128, 2, 512], F32, 2, scope=sbk)
                        SBt = sb("bSB", [128, 2, 2, 512], BF16, 4, scope=sbk)
                        for c in range(2):
                            if first:
                                memset("dve", HBF.v((A_, c, slice(0, 30)), c), 0.0)
                            else:
                                copy("dve", HBF.v((A_, c, slice(0, 30)), c), HBh[l].v((A_, c, A_)))
                        bst = {}

                        def Bdiag():
                            for c in range(2):
                                for j in range(31):
                                    dgv = V(DG.t[:, c * 31 + j, :], [DG.b[c]])
                                    cw = col(l, C_DWW + c * 31 + j)
                                    if j % 2 == 0:
                                        act(dgv, IDB.v(A_), AF.Copy, scale=cw)
                                    else:
                                        S.op("dve", lambda dgv=dgv, cw=cw: E["dve"].tensor_scalar_mul(out=dgv.ap, in0=IDB.t[:, :], scalar1=cw.ap),
                                             [IDB.v(A_), cw], [dgv])

                        def B1(t_):
                            for c in range(2):
                                bg = proj(WS1, 256 + c * 128, t_, 0)
                                act(SGB.v((A_, c, A_), c), bg.v(A_), AF.Sigmoid)
                                ba = proj(WS1, c * 128, t_, 0)
                                tt("dve", HBF.v((A_, c, slice(30 + t_ * NT, 30 + (t_ + 1) * NT)), c), ba.v(A_), SGB.v((A_, c, A_), c), ALU.mult)

                        def B2(t_):
                            for c in range(2):
                                cb = psum()
                                mm(cb.v(A_), [(V(DG.t[:, c * 31 + j, :], [DG.b[c]]), HBF.v((A_, c, slice(t_ * NT + j, t_ * NT + j + NT)), c)) for j in range(31)])
                                act(CB.v((A_, t_, c, A_), t_ * 2 + c), cb.v(A_), AF.Identity, bias=col(l, C_DWB + c))
                                act(SQB.v((A_, t_, c, A_), t_ * 2 + c), CB.v((A_, t_, c, A_), t_ * 2 + c), AF.Square)

                        def B3(t_):
                            mb = psum()
                            mm(mb.v(A_), [(ONEF.v(A_), CB.v((A_, t_, c, A_), t_ * 2 + c)) for c in range(2)])
                            qb = psum()
                            mm(qb.v(A_), [(ONEF.v(A_), SQB.v((A_, t_, c, A_), t_ * 2 + c)) for c in range(2)])
                            bst[t_] = (mb, qb)

                        def B4(t_):
                            mb, qb = bst[t_]
                            mu, rb, t1 = MU.v((A_, t_, A_), t_), RB.v((A_, t_, A_), t_), T1.v((A_, t_, A_), t_)
                            m2 = t1
                            ts("dve", mu, mb.v(A_), 1.0 / 256.0, None, ALU.mult)
                            tt("dve", m2, mu, mu, ALU.mult)
                            stt("dve", rb, qb.v(A_), 1.0 / 256.0, m2, ALU.mult, ALU.subtract)
                            act(rb, rb, AF.Ln, scale=1.0, bias=EPS_T)
                            act(rb, rb, AF.Exp, scale=-0.5)
                            for c in range(2):
                                tt("dve", t1, CB.v((A_, t_, c, A_), t_ * 2 + c), mu, ALU.subtract)
                                tt("dve", t1, t1, rb, ALU.mult)
                                act(SBt.v((A_, t_, c, A_), t_ * 2 + c), t1, AF.Silu, scale=col(l, C_LNG + c), bias=col(l, C_LNB + c))

                        def B5(t_):
                            cs = slice(t_ * NT, (t_ + 1) * NT)
                            for co in range(2):
                                pb = psum()
                                mm(pb.v(A_), [(WSM.v((A_, ci, slice(co * 128, (co + 1) * 128)), 0), SBt.v((A_, t_, ci, A_), t_ * 2 + ci)) for ci in range(2)])
                                act(YC.v((A_, 2 + co, cs), (2 + co) * 2 + t_), pb.v(A_), AF.Identity, bias=col(l, C_PWB + co))

                        B1(0)
                        B1(1)
                        Bdiag()
                        for fstage in (B2, B3, B4, B5):
                            for t_ in range(n_tiles):
                                fstage(t_)
                        for c in range(2):
                            copy("dve", HBh[l].v((A_, c, A_)), HBF.v((A_, c, slice(ST, ST + 30)), c))
                    if True:
                      if stage >= 5:
                        for c in range(2):
                            if first:
                                memset("dve", MCF.v((A_, c, slice(0, 2)), c), 0.0)
                            else:
                                copy("dve", MCF.v((A_, c, slice(0, 2)), c), MCh[l].v((A_, c, A_)))
                            for j in range(3):
                                act(V(DGC.t[:, c * 3 + j, :], DGC.b), IDB.v(A_), AF.Copy, scale=col(l, C_CCW + c * 3 + j))

                        def C1(t_):
                            for c in range(2):
                                bh = proj(WS0, 512 + c * 128, t_)
                                copy("act", HC.v((A_, c, A_), c), bh.v(A_))
                                bc_ = proj(WS0, 256 + c * 128, t_)
                                tt("dve", MCF.v((A_, c, slice(2 + t_ * NT, 2 + (t_ + 1) * NT)), c), bc_.v(A_), HC.v((A_, c, A_), c), ALU.mult)
                                bb = proj(WS0, c * 128, t_)
                                copy("act", BG.v((A_, t_, c, A_), t_ * 2 + c), bb.v(A_))

                        def C2(t_):
                            cs = slice(t_ * NT, (t_ + 1) * NT)
                            for c in range(2):
                                cb = psum()
                                mm(cb.v(A_), [(V(DGC.t[:, c * 3 + j, :], DGC.b), MCF.v((A_, c, slice(t_ * NT + j, t_ * NT + j + NT)), c)) for j in range(3)])
                                tt("dve", YC.v((A_, 4 + c, cs), (4 + c) * 2 + t_), cb.v(A_), BG.v((A_, t_, c, A_), t_ * 2 + c), ALU.mult)

                        for fstage in (C1, C2):
                            for t_ in range(n_tiles):
                                fstage(t_)
                        for c in range(2):
                            copy("dve", MCh[l].v((A_, c, A_)), MCF.v((A_, c, slice(ST, ST + 2)), c))
                    if True:
                      if stage >= 6:
                        for c in range(2):
                            if first:
                                memset("dve", UDF.v((A_, c, slice(0, 15)), c), 0.0)
                            else:
                                copy("dve", UDF.v((A_, c, slice(0, 15)), c), UDh[l].v((A_, c, A_)))

                        def D1(t_):
                            for c in range(2):
                                bu = proj(WS1, 512 + c * 128, t_, 1)
                                uf = UF.v((A_, t_, c, A_), t_ * 2 + c)
                                copy("act", uf, bu.v(A_))
                                copy("dve", UDF.v((A_, c, slice(15 + t_ * NT, 15 + (t_ + 1) * NT)), c), uf)

                        def D2(t_):
                            for c in range(2):
                                uf = UF.v((A_, t_, c, A_), t_ * 2 + c)
                                pbv = PB.v((A_, t_, c, A_), t_ * 2 + c)
                                ntap = 4 if c == 0 else 16
                                base = 0 if c == 0 else 4
                                wb = psum()
                                mm(wb.v(A_), [(V(DTAP.t[:, base + j, :], DTAP.b), UDF.v((A_, c, slice(15 + t_ * NT - j, 15 + t_ * NT - j + NT)), c)) for j in range(ntap)])
                                tt("dve", pbv, wb.v(A_), uf, ALU.subtract)
                                if first and t_ == 0:
                                    tt("dve", T16.v((A_, c, A_), c), wb.w(wb.t[:, 0:16]), DCORR.v((A_, c, A_)), ALU.mult)
                                    tt("dve", PB.v((A_, t_, c, slice(0, 16)), t_ * 2 + c), T16.v((A_, c, A_), c), UF.v((A_, t_, c, slice(0, 16)), t_ * 2 + c), ALU.subtract)

                        def D3(t_):
                            cs = slice(t_ * NT, (t_ + 1) * NT)
                            for c in range(2):
                                pb = psum()
                                mm(pb.v(A_), [(WSM.v((A_, 2 + c, slice(0, 128)), 1), PB.v((A_, t_, c, A_), t_ * 2 + c))])
                                act(YC.v((A_, 6 + c, cs), (6 + c) * 2 + t_), pb.v(A_), AF.Copy, scale=col(l, C_DSC + c))

                        for fstage in (D1, D2, D3):
                            for t_ in range(n_tiles):
                                fstage(t_)
                        for c in range(2):
                            copy("dve", UDh[l].v((A_, c, A_)), UDF.v((A_, c, slice(ST, ST + 15)), c))
                        S.barrier()
                    if debug and st == 0 and l == 0:
                        S.dma("sp", dbg_d[:, :, :], YC.t[:, :, :], sl_o, in_v=YC.va(A_))
                    S.dma("pool", WS0.t[:, :, 0:1024], w_out_d[l, :, :, :], sl_w0, out_v=WS0.v(A_))
                    with ExitStack() as so:
                      if stage >= 7:
                        YT = {"Y": sb("oY", [128, 8, 512], F32, 8, scope=so), "RS": sb("oRS", [128, 512], F32, 1, scope=so)}
                        SQ = sb("oSQ", [128, 4, 512], BF16, 4, scope=so)
                        for tt_ in range(n_tiles):
                            cs = slice(tt_ * NT, (tt_ + 1) * NT)
                            ssb = psum_hold()
                            pend = None
                            for m in range(8):
                                yb = psum()
                                mm(yb.v(A_), [(WS0.v((A_, k, slice(m * 128, (m + 1) * 128))), V(YC.t[:, k, cs], [YC.b[k * 2 + tt_]])) for k in range(8)])
                                if pend is not None:
                                    mm_acc(*pend)
                                copy("act", YT["Y"].v((A_, m, A_), m), yb.v(A_))
                                sq = SQ.v((A_, m % 4, A_), m % 4)
                                act(sq, yb.v(A_), AF.Square)
                                pend = (ssb.v(A_), ONEB.v(A_), sq, m == 0, m == 7)
                            mm_acc(*pend)
                            postnorm_residual(l, C_POST, YT, ssb, tt_)
                            psum_release(ssb)
                        S.barrier()
                S.barrier()
                if not do_ffn:
                    continue
                with ExitStack() as sf:
                    prenorm(l, C_FPRE, sf)
                    G = sb("fG", [128, 22, ST], BF16, 44, scope=sf)
                    UG = sb("fUG", [128, 2, 2, 2 + ST], BF16, 4, scope=sf)
                    DGF = sb("fDGF", [128, 2, 6, 128], BF16, 2, scope=sf)
                    SGT = sb("fSGT", [128, 2, 2, 512], F32, 4, scope=sf)
                    WU = [WS0, WS1]
                    wsl = [sl_w0, sl_w1]
                    def ffn_stage1(g, pi):
                        Wg = WU[g % 2]
                        jp = g * 2 + pi
                        r = jp % 2
                        for gu in range(2):
                            ch = jp + gu * 22
                            if gu == 1:
                                for j in range(3):
                                    act(V(DGF.t[:, r, j, :], [DGF.b[r]]), IDB.v(A_), AF.Copy, scale=col(l, C_FCW + ch * 3 + j))
                            uv = UG.v((A_, r, gu, slice(0, 2)), r * 2 + gu)
                            if first:
                                memset("dve", uv, 0.0)
                            else:
                                copy("dve", uv, V(FH[l].t[:, ch, :], FH[l].b))
                        for tt_ in range(n_tiles):
                            for gu in range(2):
                                pb = proj(Wg, gu * 256 + pi * 128, tt_)
                                uvw = UG.v((A_, r, gu, slice(2 + tt_ * NT, 2 + (tt_ + 1) * NT)), r * 2 + gu)
                                copy("act" if gu == 0 else "dve", uvw, pb.v(A_))
                            ct = SGT.v((A_, r, tt_, A_), r * 2 + tt_)
                            act(ct, UG.v((A_, r, 0, slice(tt_ * NT, tt_ * NT + NT)), r * 2), AF.Copy, scale=col(l, C_FCW + jp * 3 + 0))

                    def ffn_stage2(g, pi):
                        jp = g * 2 + pi
                        r = jp % 2
                        for tt_ in range(n_tiles):
                            ct = SGT.v((A_, r, tt_, A_), r * 2 + tt_)
                            for j in (1, 2):
                                stt("dve", ct, UG.v((A_, r, 0, slice(tt_ * NT + j, tt_ * NT + j + NT)), r * 2), col(l, C_FCW + jp * 3 + j), ct, ALU.mult, ALU.add)
                            cb = psum()
                            mm(cb.v(A_), [(V(DGF.t[:, r, j, :], [DGF.b[r]]),
                                           UG.v((A_, r, 1, slice(tt_ * NT + j, tt_ * NT + j + NT)), r * 2 + 1)) for j in range(3)])
                            act(ct, ct, AF.Silu)
                            tt("dve", G.v((A_, jp, slice(tt_ * NT, (tt_ + 1) * NT)), jp * 2 + tt_), cb.v(A_), ct, ALU.mult)
                        for gu in range(2):
                            ch = jp + gu * 22
                            copy("dve", V(FH[l].t[:, ch, :], FH[l].b), UG.v((A_, r, gu, slice(ST, ST + 2)), r * 2 + gu))

                    prev = None
                    for g in range(11):
                        Wg = WU[g % 2]
                        S.dma("pool", Wg.t[:, :, 0:256], w_up_d[l, :, :, g * 256:(g + 1) * 256], wsl[g % 2], out_v=Wg.va(A_))
                        S.dma("pool", Wg.t[:, :, 256:512], w_up_d[l, :, :, DFF + g * 256: DFF + (g + 1) * 256], wsl[g % 2], out_v=Wg.va(A_))
                        for pi in range(2):
                            ffn_stage1(g, pi)
                            if prev is not None:
                                ffn_stage2(*prev)
                            prev = (g, pi)
                    ffn_stage2(*prev)
                    YT = {"Y": sb("fY", [128, 2, 8, 512], F32, 16, scope=sf), "RS": sb("fRS", [128, 512], F32, 1, scope=sf)}
                    SQ = sb("fSQ", [128, 4, 512], BF16, 4, scope=sf)
                    ssbs = [psum_hold(), psum_hold()]
                    pend = None
                    it = 0
                    for mp in range(4):
                        Wg = WU[(mp + 1) % 2]
                        for hh in range(2):
                            S.dma("pool", Wg.t[:, :, hh * 256:(hh + 1) * 256],
                                  w_dn_d[l, :, hh * 8:(hh + 1) * 8, mp * 256:(mp + 1) * 256], wsl[(mp + 1) % 2], out_v=Wg.va(A_))
                        S.dma("pool", Wg.t[:, 0:6, 512:768], w_dn_d[l, :, 16:22, mp * 256:(mp + 1) * 256], wsl[(mp + 1) % 2], out_v=Wg.va(A_))
                        for tt_ in range(n_tiles):
                            for mi in range(2):
                                m = mp * 2 + mi
                                yb = psum()
                                pairs = []
                                for k in range(22):
                                    kk, part = k % 8, k // 8
                                    pairs.append((Wg.va((A_, kk, slice(part * 256 + mi * 128, part * 256 + (mi + 1) * 128))),
                                                  G.v((A_, k, slice(tt_ * NT, (tt_ + 1) * NT)), k * 2 + tt_)))
                                mm(yb.v(A_), pairs)
                                if pend is not None:
                                    mm_acc(*pend)
                                copy("act", V(YT["Y"].t[:, tt_, m, :], [YT["Y"].b[tt_ * 8 + m]]), yb.v(A_))
                                sq = SQ.v((A_, it % 4, A_), it % 4)
                                it += 1
                                act(sq, yb.v(A_), AF.Square)
                                pend = (ssbs[tt_].v(A_), ONEB.v(A_), sq, m == 0, m == 7)
                    mm_acc(*pend)
                    for tt_ in range(n_tiles):
                        cs = slice(tt_ * NT, (tt_ + 1) * NT)
                        RS = YT["RS"]
                        rstd_from_ss(RS.v(A_), ssbs[tt_].v(A_), float(D))
                        psum_release(ssbs[tt_])
                        for m in range(8):
                            y = V(YT["Y"].t[:, tt_, m, :], [YT["Y"].b[tt_ * 8 + m]])
                            tt("dve", y, y, RS.v(A_), ALU.mult)
                            stt("dve", X.v((A_, m, cs), m * 2 + tt_), y, col(l, C_FPOST + m), X.v((A_, m, cs), m * 2 + tt_), ALU.mult, ALU.add)
                        if l + 1 < n_layers:
                            prenorm_tile(l + 1, C_PRE, tt_)
                    S.barrier()
            for xt_ in range(ST // NT):
                xs_ = slice(xt_ * NT, (xt_ + 1) * NT)
                S.dma("sp", out_d[seq, :, t0 + xt_ * NT:t0 + (xt_ + 1) * NT].rearrange("(k p) t -> p k t", p=128), X.t[:, :, xs_], sl_o,
                      in_v=V(X.t[:, :, xs_], [X.b[k * 2 + xt_] for k in range(8)]))
        E["sp"].wait_ge(sl_o.sem, sl_o.cnt)
        S.barrier(engines=("pe", "act", "dve", "pool", "sp"))
        print(f"[build] instructions={S.nins} waits={S.nwait} sems={S.nsem}")
    return nc


def _consts():
    ident = np.eye(128, dtype=np.float32)
    s = np.arange(64)[:, None]
    t = np.arange(64)[None, :]
    m = (s <= t).astype(np.float32)
    maskT = np.tile(m, (1, 8))
    p = np.arange(128)
    bmask = ((p[:, None] // 64) == (p[None, :] // 64)).astype(np.float32)
    rmask = np.ones((128, 512), np.float32)
    rmask[:, ::64] = 0.0
    wins = (2, 4, 8, 16)
    dtap = np.zeros((128, 20, 128), np.float32)
    dcorr = np.ones((128, 2, 16), np.float32)
    for c in range(2):
        base = 0 if c == 0 else 4
        ntap = 4 if c == 0 else 16
        for pp in range(128):
            w = wins[c * 2 + pp // 64]
            for j in range(ntap):
                if j < w:
                    dtap[pp, base + j, pp] = 1.0 / w
            for tt_ in range(16):
                dcorr[pp, c, tt_] = w / min(tt_ + 1.0, float(w))
    hm8 = np.zeros((128, 2), np.float32)
    hm8[:64, 0] = 0.125
    hm8[64:, 1] = 0.125
    return dict(c_ident=ident, c_maskT=maskT, c_bmask=bmask, c_rmask=rmask, c_dtap=dtap, c_dcorr=dcorr, c_hm8=hm8)


def _prep_weights(w_in, lb_gamma, a_norm_g, b_dw_w, b_dw_b, b_ln_g, b_ln_b, b_pw_w, b_pw_b,
                  c_conv_w, d_proj, d_scale, w_out, mix_pre_g, mix_post_g, ffn_pre_g, ffn_post_g,
                  w_up, ffn_conv_w, w_down):
    f = lambda a: np.ascontiguousarray(np.asarray(a, dtype=np.float32))
    L = DEPTH
    kp = lambda w, nk: f(np.asarray(w, np.float32).reshape(L, nk, 128, -1).transpose(0, 2, 1, 3))
    out = {}
    out["w_in_r"] = kp(w_in, 8)
    out["w_out_r"] = kp(w_out, 8)
    out["w_up_r"] = kp(w_up, 8)
    out["w_down_r"] = kp(w_down, 22)
    out["b_pw_r"] = kp(b_pw_w, 2)
    dp = np.zeros((L, 128, 2, 128), np.float32)
    dpj = np.asarray(d_proj, np.float32)
    for c in range(2):
        for gg in range(2):
            dp[:, gg * 64:(gg + 1) * 64, c, gg * 64:(gg + 1) * 64] = dpj[:, c * 2 + gg]
    out["d_proj_r"] = dp
    cols = np.zeros((L, 128, NCOL), np.float32)

    def put(base, vec, n):
        v = np.asarray(vec, np.float32).reshape(L, n, 128)
        for i in range(n):
            cols[:, :, base + i] = v[:, i]

    put(C_PRE, mix_pre_g, 8)
    put(C_POST, mix_post_g, 8)
    put(C_FPRE, ffn_pre_g, 8)
    put(C_FPOST, ffn_post_g, 8)
    put(C_ANG, a_norm_g, 2)
    put(C_DWB, b_dw_b, 2)
    put(C_LNG, b_ln_g, 2)
    put(C_LNB, b_ln_b, 2)
    put(C_PWB, b_pw_b, 2)
    put(C_DSC, d_scale, 2)
    put(C_LBG, lb_gamma, 2)
    dww = np.asarray(b_dw_w, np.float32).reshape(L, 31, 2, 128)
    for c in range(2):
        for j in range(31):
            cols[:, :, C_DWW + c * 31 + j] = dww[:, j, c]
    ccw = np.asarray(c_conv_w, np.float32).reshape(L, 3, 2, 128)
    for c in range(2):
        for j in range(3):
            cols[:, :, C_CCW + c * 3 + j] = ccw[:, j, c]
    fcw = np.asarray(ffn_conv_w, np.float32).reshape(L, 3, 44, 128)
    for ch in range(44):
        for j in range(3):
            cols[:, :, C_FCW + ch * 3 + j] = fcw[:, j, ch]
    out["cols"] = cols
    out.update(_consts())
    return out


_NC_CACHE = {}


def kernel(x, **weights):
    x = np.asarray(x, dtype=np.float32)
    shared = _prep_weights(**weights)
    if "nc" not in _NC_CACHE:
        _NC_CACHE["nc"] = build_program()
    nc = _NC_CACHE["nc"]
    in_maps = []
    for i in range(NCORES):
        xs = np.ascontiguousarray(x[2 * i:2 * i + 2].transpose(0, 2, 1))
        m = {"x_fm": xs}
        m.update(shared)
        in_maps.append(m)
    res = run_bass_kernel_spmd(nc, in_maps, core_ids=list(range(NCORES)))
    outs = [np.asarray(r["out_fm"], dtype=np.float32).transpose(0, 2, 1) for r in res.results]
    return np.ascontiguousarray(np.concatenate(outs, axis=0))
```

```python
import math
from contextlib import ExitStack

import numpy as np
import concourse.bass as bass
import concourse.mybir as mybir
from concourse.bass_utils import run_bass_kernel_spmd

F32 = mybir.dt.float32
BF16 = mybir.dt.bfloat16
ALU = mybir.AluOpType
AF = mybir.ActivationFunctionType
AX = mybir.AxisListType

NCORES = 8
D = 1024
SEQ = 2048
BATCH = 16
DEPTH = 2
DFF = 2816
NT = 512
ST = 1024
EPS = 1e-6
LN_MINF = math.log(1e-30)

C_PRE, C_POST, C_FPRE, C_FPOST = 0, 8, 16, 24
C_ANG, C_DWB, C_LNG, C_LNB, C_PWB, C_DSC, C_LBG = 32, 34, 36, 38, 40, 42, 44
C_DWW = 46
C_CCW = 108
C_FCW = 114
NCOL = C_FCW + 44 * 3


class Buf:
    __slots__ = ("w", "r", "name")

    def __init__(self, name=""):
        self.w = None
        self.r = {}
        self.name = name


class V:
    __slots__ = ("ap", "bufs", "tok")

    def __init__(self, ap, bufs, tok=None):
        self.ap = ap
        self.bufs = bufs
        self.tok = tok


class BankRef:
    def __init__(self, bank):
        self.bank = bank
        self.t = bank.t
        self.b = bank.b
        self.token = bank.token

    def v(self, idx=slice(None), slot=0):
        return V(self.t[idx], [self.b[0]], (self.bank, self.token))

    def w(self, ap):
        return V(ap, [self.b[0]], (self.bank, self.token))


class T:
    def __init__(self, t, nslots=1, name=""):
        self.t = t
        self.b = [Buf(f"{name}{i}") for i in range(nslots)]

    def v(self, idx, slot=0):
        return V(self.t[idx], [self.b[slot]])

    def va(self, idx):
        return V(self.t[idx], list(self.b))


class Slot:
    def __init__(self, sem):
        self.sem = sem
        self.cnt = 0


class Sched:
    EPOCH = 30000

    def __init__(self, nc, es):
        self.nc = nc
        self.es = es
        self.E = {"pe": nc.tensor, "act": nc.scalar, "dve": nc.vector, "pool": nc.gpsimd, "sp": nc.sync}
        self.sem = {}
        self.cnt = {}
        self.nsem = 0
        self.waited = {e: {} for e in self.E}
        self.last = {}
        self.slots = []
        self.nwait = 0
        self.nins = 0

    def _newsem(self, name):
        self.nsem += 1
        return self.es.enter_context(self.nc.semaphore(f"{name}{self.nsem}"))

    def slot(self, name):
        s = Slot(self._newsem("d" + name))
        self.slots.append(s)
        return s

    def _ticket(self, e):
        if e not in self.sem or self.cnt[e] >= self.EPOCH:
            self.sem[e] = self._newsem("e" + e)
            self.cnt[e] = 0
        self.cnt[e] += 1
        t = (self.sem[e], self.cnt[e], e)
        self.last[e] = t
        return t

    def _deps(self, e, reads, writes):
        deps = {}

        def add(t):
            if t is None:
                return
            k = id(t[0])
            if k not in deps or deps[k][1] < t[1]:
                deps[k] = t

        for v in reads:
            for b in v.bufs:
                if b.w is not None and not (b.w[2] == e and e == "pe"):
                    add(b.w)
        for v in writes:
            for b in v.bufs:
                if b.w is not None and b.w[2] != e:
                    add(b.w)
                for k, t in b.r.items():
                    if t[2] != e:
                        add(t)
        return deps

    def _emit_waits(self, e, deps):
        eng = self.E[e]
        w = self.waited[e]
        for k, t in deps.items():
            if w.get(k, 0) >= t[1]:
                continue
            eng.wait_ge(t[0], t[1])
            w[k] = t[1]
            self.nwait += 1

    def op(self, e, fn, reads=(), writes=()):
        for v in list(reads) + list(writes):
            if v.tok is not None and v.tok[0].token is not v.tok[1]:
                raise RuntimeError("stale PSUM bank view (bank re-allocated before this use was issued)")
        self._emit_waits(e, self._deps(e, reads, writes))
        ins = fn()
        t = self._ticket(e)
        ins.then_inc(t[0], 1)
        self.nins += 1
        for v in writes:
            for b in v.bufs:
                b.w = t
                b.r = {}
        for v in reads:
            for b in v.bufs:
                b.r[e] = t
        return t

    def dma(self, e, out, in_, slot, out_v=None, in_v=None):
        reads = [in_v] if in_v is not None else []
        writes = [out_v] if out_v is not None else []
        deps = {}
        for v in reads:
            for b in v.bufs:
                if b.w is not None:
                    deps[id(b.w[0])] = b.w
        for v in writes:
            for b in v.bufs:
                if b.w is not None and b.w[0] is not slot.sem:
                    deps[id(b.w[0])] = b.w
                for k, t in b.r.items():
                    kk = id(t[0])
                    if kk not in deps or deps[kk][1] < t[1]:
                        deps[kk] = t
        self._emit_waits(e, deps)
        self.E[e].dma_start(out=out, in_=in_).then_inc(slot.sem, 16)
        slot.cnt += 16
        t = (slot.sem, slot.cnt, "dma")
        for v in writes:
            for b in v.bufs:
                b.w = t
                b.r = {}
        for v in reads:
            for b in v.bufs:
                b.r["dma" + str(id(slot))] = t
        return t

    def barrier(self, engines=("act", "dve")):
        ts = [t for k, t in self.last.items() if k in ("pe", "act", "dve", "pool")]
        for e in engines:
            deps = {}
            for t in ts:
                if t[2] == e:
                    continue
                deps[id(t[0])] = t
            self._emit_waits(e, deps)


def build_program(n_layers=DEPTH, n_st=4, do_ffn=True, stage=99, debug=False):
    nc = bass.Bass("TRN2", target_bir_lowering=False)
    dr = lambda name, shape: nc.dram_tensor(name, shape, F32, kind="ExternalInput").ap()
    x_d = dr("x_fm", [2, D, SEQ])
    w_in_d = dr("w_in_r", [DEPTH, 128, 8, 2560])
    w_out_d = dr("w_out_r", [DEPTH, 128, 8, D])
    w_up_d = dr("w_up_r", [DEPTH, 128, 8, 2 * DFF])
    w_dn_d = dr("w_down_r", [DEPTH, 128, 22, D])
    bpw_d = dr("b_pw_r", [DEPTH, 128, 2, 256])
    dpj_d = dr("d_proj_r", [DEPTH, 128, 2, 128])
    cols_d = dr("cols", [DEPTH, 128, NCOL])
    ident_d = dr("c_ident", [128, 128])
    maskT_d = dr("c_maskT", [64, 512])
    bmask_d = dr("c_bmask", [128, 128])
    rmask_d = dr("c_rmask", [128, 512])
    dtap_d = dr("c_dtap", [128, 20, 128])
    dcorr_d = dr("c_dcorr", [128, 2, 16])
    hm8_d = dr("c_hm8", [128, 2])
    out_d = nc.dram_tensor("out_fm", [2, D, SEQ], F32, kind="ExternalOutput").ap()
    dbg_d = nc.dram_tensor("dbg", [128, 8, ST], BF16, kind="ExternalOutput").ap() if debug else None

    with ExitStack() as es:
        S = Sched(nc, es)
        E = S.E

        uid = [0]

        def sb(name, shape, dt, nslots=1, scope=es):
            uid[0] += 1
            return T(scope.enter_context(nc.sbuf_tensor(f"{name}_{uid[0]}", shape, dt)), nslots, name)

        banks = [T(es.enter_context(nc.psum_tensor(f"ps{i}", [128, 512], F32)), 1, f"ps{i}") for i in range(8)]
        rotation = list(banks)

        def psum():
            b = rotation.pop(0)
            rotation.append(b)
            b.token = object()
            return BankRef(b)

        def psum_hold():
            b = rotation.pop(0)
            b.token = object()
            return BankRef(b)

        def psum_release(ref):
            rotation.append(ref.bank)

        X = sb("X", [128, 8, ST], F32, 16)
        H = sb("H", [128, 8, ST], BF16, 2)
        COLS = sb("COLS", [128, DEPTH, NCOL], F32)
        LB = sb("LB", [128, 3, DEPTH, 2], F32)
        IDB = sb("IDB", [128, 128], BF16)
        ONEB = sb("ONEB", [128, 128], BF16)
        ONEF = sb("ONEF", [128, 128], F32)
        MASKT = sb("MASKT", [64, 512], F32)
        BMASK = sb("BMASK", [128, 128], F32)
        RMASK = sb("RMASK", [128, 512], F32)
        DTAP = sb("DTAP", [128, 20, 128], BF16)
        DCORR = sb("DCORR", [128, 2, 16], F32)
        HM8 = sb("HM8", [128, 2], F32)
        HBh = [sb(f"HBh{l}", [128, 2, 30], BF16) for l in range(DEPTH)]
        MCh = [sb(f"MCh{l}", [128, 2, 2], BF16) for l in range(DEPTH)]
        UDh = [sb(f"UDh{l}", [128, 2, 15], BF16) for l in range(DEPTH)]
        FH = [sb(f"FH{l}", [128, 44, 2], BF16) for l in range(DEPTH)]
        S32 = [sb(f"S32_{l}", [128, 2, 128], F32, 2) for l in range(DEPTH)]
        SBF = [sb(f"SBF{l}", [128, 2, 2, 128], BF16, 4) for l in range(DEPTH)]
        WS0 = sb("WS0", [128, 8, 1024], BF16)
        WS1 = sb("WS1", [128, 8, 768], BF16, 2)
        WSM = sb("WSM", [128, 4, 256], BF16, 2)

        sl_c = S.slot("c")
        sl_x = S.slot("x")
        sl_o = S.slot("o")
        sl_w0 = S.slot("w0")
        sl_w1 = S.slot("w1")
        sl_wm = S.slot("wm")

        def act(out, in_, func, reads=None, scale=1.0, bias=0.0, extra_reads=()):
            rd = [in_] + list(extra_reads)
            kw = {}
            if isinstance(scale, V):
                rd.append(scale)
                kw["scale"] = scale.ap
            else:
                kw["scale"] = float(scale)
            if isinstance(bias, V):
                rd.append(bias)
                kw["bias"] = bias.ap
            elif bias != 0.0:
                kw["bias"] = float(bias)
            return S.op("act", lambda: E["act"].activation(out=out.ap, in_=in_.ap, func=func, **kw), rd, [out])

        def tt(e, out, a, b, op):
            return S.op(e, lambda: E[e].tensor_tensor(out=out.ap, in0=a.ap, in1=b.ap, op=op), [a, b], [out])

        def ts(e, out, a, s1, s2, op0, op1=None):
            rd = [a]
            k1 = s1.ap if isinstance(s1, V) else float(s1)
            if isinstance(s1, V):
                rd.append(s1)
            if s2 is None:
                return S.op(e, lambda: E[e].tensor_scalar(out=out.ap, in0=a.ap, scalar1=k1, scalar2=0.0, op0=op0, op1=ALU.add), rd, [out])
            k2 = s2.ap if isinstance(s2, V) else float(s2)
            if isinstance(s2, V):
                rd.append(s2)
            return S.op(e, lambda: E[e].tensor_scalar(out=out.ap, in0=a.ap, scalar1=k1, scalar2=k2, op0=op0, op1=op1), rd, [out])

        def stt(e, out, a, s, b, op0, op1):
            rd = [a, b]
            k = s.ap if isinstance(s, V) else float(s)
            if isinstance(s, V):
                rd.append(s)
            return S.op(e, lambda: E[e].scalar_tensor_tensor(out=out.ap, in0=a.ap, scalar=k, in1=b.ap, op0=op0, op1=op1), rd, [out])

        def copy(e, out, in_):
            if e == "act":
                return act(out, in_, AF.Copy)
            return S.op(e, lambda: E[e].tensor_copy(out=out.ap, in_=in_.ap), [in_], [out])

        def memset(e, out, val):
            return S.op(e, lambda: E[e].memset(out.ap, val), [], [out])

        def mm(out, pairs):
            rd = []
            for l, r in pairs:
                rd += [l, r]
            n = len(pairs)

            def fn():
                ins = None
                for i, (l, r) in enumerate(pairs):
                    ins = E["pe"].matmul(out.ap, lhsT=l.ap, rhs=r.ap, start=(i == 0), stop=(i == n - 1))
                return ins

            return S.op("pe", fn, rd, [out])

        def col(l, c):
            return COLS.v((slice(None), l, slice(c, c + 1)))

        A_ = slice(None)

        S.dma("sp", COLS.t[:, :, :], cols_d.rearrange("l p n -> p l n"), sl_c, out_v=COLS.v(A_))
        S.dma("pool", IDB.t[:, :], ident_d[:, :], sl_c, out_v=IDB.v(A_))
        S.dma("sp", MASKT.t[:, :], maskT_d[:, :], sl_c, out_v=MASKT.v(A_))
        S.dma("sp", BMASK.t[:, :], bmask_d[:, :], sl_c, out_v=BMASK.v(A_))
        S.dma("sp", RMASK.t[:, :], rmask_d[:, :], sl_c, out_v=RMASK.v(A_))
        S.dma("pool", DTAP.t[:, :, :], dtap_d[:, :, :], sl_c, out_v=DTAP.v(A_))
        S.dma("sp", DCORR.t[:, :, :], dcorr_d[:, :, :], sl_c, out_v=DCORR.v(A_))
        S.dma("sp", HM8.t[:, :], hm8_d[:, :], sl_c, out_v=HM8.v(A_))
        memset("dve", ONEB.v(A_), 1.0)
        memset("dve", ONEF.v(A_), 1.0)
        with ExitStack() as ps:
            EX = sb("lbEX", [128, DEPTH, 2], F32, scope=ps)
            SM = sb("lbSM", [128, 2], F32, scope=ps)
            LS = sb("lbLS", [128, DEPTH, 2], F32, scope=ps)
            CS = sb("lbCS", [128, 2], F32, scope=ps)
            for l in range(DEPTH):
                act(EX.v((A_, l, A_)), COLS.v((A_, l, slice(C_LBG, C_LBG + 2))), AF.Exp)
            tt("dve", SM.v(A_), EX.v((A_, 0, A_)), EX.v((A_, 1, A_)), ALU.add)
            S.op("dve", lambda: E["dve"].reciprocal(out=SM.t[:, :], in_=SM.t[:, :]), [SM.v(A_)], [SM.v(A_)])
            for l in range(DEPTH):
                tt("dve", LS.v((A_, l, A_)), EX.v((A_, l, A_)), SM.v(A_), ALU.mult)
            copy("dve", CS.v(A_), LS.v((A_, 0, A_)))
            tt("dve", LB.v((A_, 0, 0, A_)), CS.v(A_), LS.v((A_, 0, A_)), ALU.subtract)
            for l in range(1, DEPTH):
                tt("dve", CS.v(A_), CS.v(A_), LS.v((A_, l, A_)), ALU.add)
                tt("dve", LB.v((A_, 0, l, A_)), CS.v(A_), LS.v((A_, 0, A_)), ALU.subtract)
            ts("dve", LB.v((A_, 1, A_, A_)), LB.v((A_, 0, A_, A_)), -1.0, 1.0, ALU.mult, ALU.add)
            ts("dve", LB.v((A_, 2, A_, A_)), LB.v((A_, 1, A_, A_)), -1.0, None, ALU.mult)
            S.barrier()

        def lbv(kind, l, c):
            return LB.v((A_, kind, l, slice(c, c + 1)))

        def rstd_from_ss(RSTD, ss_bank, n):
            act(RSTD, ss_bank, AF.Ln, scale=1.0 / n, bias=EPS_T)
            act(RSTD, RSTD, AF.Exp, scale=-0.5)

        EPSC = sb("EPSC", [128, 1], F32)
        memset("dve", EPSC.v(A_), EPS)
        EPS_T = EPSC.v(A_)

        def prenorm(l, gcol, scope):
            SQ = sb("pnSQ", [128, 2, 512], BF16, 2, scope=scope)
            RS = sb("pnRS", [128, 512], F32, 1, scope=scope)
            for tt_ in range(ST // NT):
                cs = slice(tt_ * NT, (tt_ + 1) * NT)
                ssb = psum_hold()
                for k in range(8):
                    sq = SQ.v((A_, k % 2, A_), k % 2)
                    act(sq, X.v((A_, k, cs), k * 2 + tt_), AF.Square)
                    mm_acc(ssb.v(A_), ONEB.v(A_), sq, k == 0, k == 7)
                rstd_from_ss(RS.v(A_), ssb.v(A_), float(D))
                psum_release(ssb)
                for k in range(8):
                    stt("dve", H.v((A_, k, cs), tt_), X.v((A_, k, cs), k * 2 + tt_), col(l, gcol + k), RS.v(A_), ALU.mult, ALU.mult)

        def mm_acc(out, l, r, start, stop):
            return S.op("pe", lambda: E["pe"].matmul(out.ap, lhsT=l.ap, rhs=r.ap, start=start, stop=stop), [l, r], [out])

        def proj(W, c0, tt_, wslot=None):
            cs = slice(tt_ * NT, (tt_ + 1) * NT)
            b = psum()
            wv = (lambda i: W.va(i)) if wslot is None else (lambda i: W.v(i, wslot))
            mm(b.v(A_), [(wv((A_, k, slice(c0, c0 + 128))), H.v((A_, k, cs), tt_)) for k in range(8)])
            return b

        def postnorm_residual(l, gcol, YT, SSB, tt_):
            cs = slice(tt_ * NT, (tt_ + 1) * NT)
            RS = YT["RS"]
            rstd_from_ss(RS.v(A_), SSB.v(A_), float(D))
            for m in range(8):
                y = YT["Y"].v((A_, m, A_), m)
                tt("dve", y, y, RS.v(A_), ALU.mult)
                stt("dve", X.v((A_, m, cs), m * 2 + tt_), y, col(l, gcol + m), X.v((A_, m, cs), m * 2 + tt_), ALU.mult, ALU.add)

        n_tiles = ST // NT
        for st in range(n_st):
            seq, half = st // 2, st % 2
            first = half == 0
            t0 = half * ST
            for xt_ in range(ST // NT):
                xs_ = slice(xt_ * NT, (xt_ + 1) * NT)
                S.dma("sp", X.t[:, :, xs_], x_d[seq, :, t0 + xt_ * NT:t0 + (xt_ + 1) * NT].rearrange("(k p) t -> p k t", p=128), sl_x,
                      out_v=V(X.t[:, :, xs_], [X.b[k * 2 + xt_] for k in range(8)]))
            for l in range(n_layers):
                with ExitStack() as mx:
                    YC = sb("YC", [128, 8, ST], BF16, 16, scope=mx)
                    if stage >= 2:
                        prenorm(l, C_PRE, mx)
                    S.dma("pool", WS0.t[:, :, 0:1024], w_in_d[l, :, :, 0:1024], sl_w0, out_v=WS0.v(A_))
                    with ExitStack() as sa:
                      if stage >= 3:
                        TMP = sb("aTMP", [128, 7, 512], F32, 7, scope=sa)
                        QT = sb("aQT", [128, 2, 2, 2, 512], BF16, 8, scope=sa)
                        KT = sb("aKT", [128, 2, 2, 512], BF16, 4, scope=sa)
                        VF = sb("aVF", [128, 2, 2, 512], BF16, 4, scope=sa)
                        SG = sb("aSG", [128, 2, 2, 512], F32, 4, scope=sa)
                        DL = sb("aDL", [128, 2, 2, 8], F32, 4, scope=sa)
                        KV = sb("aKV", [64, 2, 8, 512], BF16, 16, scope=sa)
                        PT = sb("aPT", [64, 2, 4, 512], BF16, 8, scope=sa)
                        DB = sb("aDB", [128, 8, 2, 128], F32, 8, scope=sa)
                        ON = sb("aON", [64, 4, 512], BF16, 4, scope=sa)
                        OSQ = sb("aOSQ", [64, 4, 512], F32, 4, scope=sa)
                        MS = sb("aMS", [64, 4, 8], F32, 4, scope=sa)
                        STMP = sb("aSTMP", [128, 2, 128], F32, 1, scope=sa)
                        SBS = sb("aSBS", [128, 2, 9, 2, 128], BF16, 18, scope=sa)
                        if first:
                            for c in range(2):
                                memset("dve", S32[l].v((A_, c, A_), c), 0.0)
                                memset("dve", SBF[l].v((A_, 0, c, A_), c), 0.0)
                        ast = {}

                        def A1(t_):
                            QSs = [TMP.v((A_, i, A_), i) for i in (0, 1)]
                            SIGs = [TMP.v((A_, i, A_), i) for i in (2, 3)]
                            LOGF, KK, BC = [TMP.v((A_, i, A_), i) for i in (4, 5, 6)]
                            EB = LOGF
                            for c in range(2):
                                b = proj(WS0, c * 128, t_)
                                act(QSs[c], b.v(A_), AF.Silu)
                            for c in range(2):
                                b = proj(WS0, 768 + c * 128, t_)
                                act(SG.v((A_, t_, c, A_), t_ * 2 + c), b.v(A_), AF.Silu)
                            for c in range(2):
                                b = proj(WS0, 256 + c * 128, t_)
                                act(SIGs[c], b.v(A_), AF.Sigmoid)
                            for c in range(2):
                                b = proj(WS0, 512 + c * 128, t_)
                                copy("act", VF.v((A_, t_, c, A_), t_ * 2 + c), b.v(A_))
                            for c in range(2):
                                QS, SIG = QSs[c], SIGs[c]
                                ENB = SIG
                                act(LOGF, SIG, AF.Ln, scale=lbv(1, l, c), bias=lbv(0, l, c))
                                ts("dve", LOGF, LOGF, LN_MINF, None, ALU.max)
                                ts("dve", KK, SIG, lbv(2, l, c), lbv(1, l, c), ALU.mult, ALU.add)
                                S.op("dve", lambda: E["dve"].tensor_tensor_scan(
                                    out=BC.ap, data0=RMASK.t[:, :], data1=LOGF.ap, initial=0.0, op0=ALU.mult, op1=ALU.add),
                                    [RMASK.v(A_), LOGF], [BC])
                                act(EB, BC, AF.Exp)
                                act(ENB, BC, AF.Exp, scale=-1.0)
                                for hh in range(2):
                                    stt("dve", QT.v((A_, t_, c, hh, A_), t_ * 4 + c * 2 + hh), QS, HM8.v((A_, slice(hh, hh + 1))), EB, ALU.mult, ALU.mult)
                                tt("dve", KT.v((A_, t_, c, A_), t_ * 2 + c), KK, ENB, ALU.mult)
                                copy("dve", DL.v((A_, t_, c, A_), t_ * 2 + c), V(TMP.t[:, 4, 63:512:64], [TMP.b[4]]))

                        def A2(t_):
                            for j in range(8):
                                js = slice(j * 64, (j + 1) * 64)
                                b = psum()
                                srcs = [KT.v((A_, t_, 0, js), t_ * 2), KT.v((A_, t_, 1, js), t_ * 2 + 1),
                                        VF.v((A_, t_, 0, js), t_ * 2), VF.v((A_, t_, 1, js), t_ * 2 + 1)]

                                def fn(b=b, srcs=srcs):
                                    ins = None
                                    for i, s_ in enumerate(srcs):
                                        ins = E["pe"].matmul(b.t[0:64, i * 128:(i + 1) * 128], lhsT=s_.ap, rhs=IDB.t[:, :], start=True, stop=True)
                                    return ins

                                S.op("pe", fn, srcs + [IDB.v(A_)], [b.v(A_)])
                                copy("act", KV.v((slice(0, 64), t_, j, A_), t_ * 8 + j), b.w(b.t[0:64, :]))
                            spb = []
                            for jp in range(4):
                                b = psum()

                                def fn(b=b, jp=jp):
                                    ins = None
                                    for jj in range(2):
                                        j = jp * 2 + jj
                                        for c in range(2):
                                            ins = E["pe"].matmul(b.t[:, (jj * 2 + c) * 128:(jj * 2 + c + 1) * 128],
                                                                 lhsT=KV.t[0:64, t_, j, c * 128:(c + 1) * 128],
                                                                 rhs=KV.t[0:64, t_, j, 256 + c * 128: 256 + (c + 1) * 128], start=True, stop=True)
                                    return ins

                                S.op("pe", fn, [KV.v(A_, t_ * 8 + jp * 2), KV.v(A_, t_ * 8 + jp * 2 + 1)], [b.v(A_)])
                                spb.append(b)
                            ast[t_] = spb

                        def A3(t_):
                            for jp in range(4):
                                b = psum()
                                rd = [KT.v(A_, t_ * 2), KT.v(A_, t_ * 2 + 1)] + [QT.v(A_, t_ * 4 + i) for i in range(4)]

                                def fn(b=b, jp=jp):
                                    ins = None
                                    for jj in range(2):
                                        j = jp * 2 + jj
                                        js = slice(j * 64, (j + 1) * 64)
                                        for h in range(4):
                                            c, hh = h // 2, h % 2
                                            ins = E["pe"].matmul(b.t[0:64, jj * 256 + h * 64: jj * 256 + (h + 1) * 64],
                                                                 lhsT=KT.t[:, t_, c, js], rhs=QT.t[:, t_, c, hh, js],
                                                                 start=True, stop=True)
                                    return ins

                                S.op("pe", fn, rd, [b.v(A_)])
                                tt("dve", PT.v((slice(0, 64), t_, jp, A_), t_ * 4 + jp), b.w(b.t[0:64, :]), MASKT.v(A_), ALU.mult)

                        def A4(t_):
                            spb = ast[t_]
                            for j in range(8):
                                for c in range(2):
                                    dlv = DL.v((A_, t_, c, slice(j, j + 1)), t_ * 2 + c)
                                    S.op("dve", lambda j=j, c=c, dlv=dlv: E["dve"].tensor_scalar_mul(out=DB.t[:, j, c, :], in0=BMASK.t[:, :], scalar1=dlv.ap),
                                         [BMASK.v(A_), dlv], [DB.v(A_, j)])
                            s32 = S32[l].va(A_)
                            copy("act", SBS.v((A_, t_, 0, A_, A_), t_ * 9), SBF[l].va((A_, 0, A_, A_)))
                            for j in range(8):
                                jp, jj = j // 2, j % 2
                                spv = spb[jp].w(spb[jp].t[:, jj * 256:(jj + 1) * 256].rearrange("p (c v) -> p c v", c=2))
                                tt("dve", STMP.v(A_), spv, s32, ALU.add)
                                tt("dve", s32, STMP.v(A_), DB.v((A_, j, A_, A_), j), ALU.mult)
                                copy("act", SBS.v((A_, t_, j + 1, A_, A_), t_ * 9 + j + 1), s32)
                            copy("act", SBF[l].va((A_, 0, A_, A_)), s32)

                        def A5(t_):
                            obs = []
                            for jp in range(4):
                                ob = psum()
                                for jj in range(2):
                                    j = jp * 2 + jj
                                    js = slice(j * 64, (j + 1) * 64)

                                    def fn(ob=ob, jp=jp, jj=jj, j=j, js=js):
                                        ins = None
                                        for h in range(4):
                                            c, hp = h // 2, (h % 2) * 64
                                            o_ap = ob.t[0:64, jj * 256 + h * 64: jj * 256 + (h + 1) * 64]
                                            E["pe"].matmul(o_ap, lhsT=PT.t[0:64, t_, jp, jj * 256 + h * 64: jj * 256 + (h + 1) * 64],
                                                           rhs=KV.t[0:64, t_, j, 256 + h * 64: 256 + (h + 1) * 64], start=True, stop=False)
                                            ins = E["pe"].matmul(o_ap, lhsT=QT.t[:, t_, c, h % 2, js],
                                                                 rhs=SBS.t[:, t_, j, c, hp:hp + 64], start=False, stop=True)
                                        return ins

                                    S.op("pe", fn, [PT.v(A_, t_ * 4 + jp), KV.v(A_, t_ * 8 + j)] + [QT.v(A_, t_ * 4 + i) for i in range(4)]
                                         + [SBS.v(A_, t_ * 9 + j)], [ob.v(A_)])
                                obs.append(ob)
                            for jp in range(4):
                                act(OSQ.v((A_, jp, A_), jp), obs[jp].w(obs[jp].t[0:64, :]), AF.Square)
                            for jp in range(4):
                                S.op("dve", lambda jp=jp: E["dve"].tensor_reduce(
                                    out=MS.t[:, jp, :], in_=OSQ.t[:, jp, :].rearrange("p (g v) -> p g v", v=64), axis=AX.X, op=ALU.add),
                                    [OSQ.v(A_, jp)], [MS.v(A_, jp)])
                            for jp in range(4):
                                msv = MS.v((A_, jp, A_), jp)
                                act(msv, msv, AF.Ln, scale=1.0 / 64.0, bias=V(EPSC.t[0:64, :], EPSC.b))
                                act(msv, msv, AF.Exp, scale=-0.5)
                            for jp in range(4):
                                ob = obs[jp]
                                S.op("dve", lambda ob=ob, jp=jp: E["dve"].tensor_tensor(
                                    out=ON.t[0:64, jp, :].rearrange("p (g v) -> p g v", v=64),
                                    in0=ob.t[0:64, :].rearrange("p (g v) -> p g v", v=64),
                                    in1=MS.t[:, jp, :].unsqueeze(2).to_broadcast([64, 8, 64]), op=ALU.mult),
                                    [ob.w(ob.t[0:64, :]), MS.v(A_, jp)], [ON.v(A_, jp)])

                        def A6(t_):
                            cs = slice(t_ * NT, (t_ + 1) * NT)
                            for c in range(2):
                                yb = psum()

                                def fn(yb=yb, c=c):
                                    ins = None
                                    for j in range(8):
                                        jp, jj = j // 2, j % 2
                                        ins = E["pe"].matmul(yb.t[:, j * 64:(j + 1) * 64],
                                                             lhsT=ON.t[0:64, jp, jj * 256 + c * 128: jj * 256 + (c + 1) * 128],
                                                             rhs=IDB.t[0:64, 0:64], start=True, stop=True)
                                    return ins

                                S.op("pe", fn, [ON.va(A_), IDB.v(A_)], [yb.v(A_)])
                                stt("dve", YC.v((A_, c, cs), c * 2 + t_), yb.v(A_), col(l, C_ANG + c), SG.v((A_, t_, c, A_), t_ * 2 + c), ALU.mult, ALU.mult)

                        A1(0)
                        A1(1)
                        if stage >= 3.2:
                            A2(0)
                            A3(0)
                            A4(0)
                            A2(1)
                            A3(1)
                            A4(1)
                            A5(0)
                            A6(0)
                            A5(1)
                            A6(1)
                        S.barrier()
                    S.dma("pool", WS1.t[:, :, 0:512], w_in_d[l, :, :, 1024:1536], sl_w1, out_v=WS1.v(A_, 0))
                    S.dma("pool", WSM.t[:, 0:2, :], bpw_d[l, :, :, :], sl_wm, out_v=WSM.v(A_, 0))
                    S.dma("pool", WS0.t[:, :, 0:768], w_in_d[l, :, :, 1536:2304], sl_w0, out_v=WS0.v(A_))
                    S.dma("pool", WS1.t[:, :, 512:768], w_in_d[l, :, :, 2304:2560], sl_w1, out_v=WS1.v(A_, 1))
                    S.dma("pool", WSM.t[:, 2:4, 0:128], dpj_d[l, :, :, :], sl_wm, out_v=WSM.v(A_, 1))
                    MCF = sb("cMCF", [128, 2, 2 + ST], BF16, 2, scope=mx)
                    DGC = sb("cDG", [128, 6, 128], BF16, 1, scope=mx)
                    HC = sb("cHC", [128, 2, 512], F32, 2, scope=mx)
                    BG = sb("cBG", [128, 2, 2, 512], F32, 4, scope=mx)
                    UDF = sb("dUDF", [128, 2, 15 + ST], BF16, 2, scope=mx)
                    UF = sb("dUF", [128, 2, 2, 512], F32, 4, scope=mx)
                    PB = sb("dPB", [128, 2, 2, 512], BF16, 4, scope=mx)
                    T16 = sb("dT16", [128, 2, 16], F32, 2, scope=mx)
                    with ExitStack() as sbk:
                      if stage >= 4:
                        HBF = sb("bHBF", [128, 2, 30 + ST], BF16, 2, scope=sbk)
                        DG = sb("bDG", [128, 62, 128], BF16, 2, scope=sbk)
                        SGB = sb("bSGB", [128, 2, 512], F32, 2, scope=sbk)
                        CB = sb("bCB", [128, 2, 2, 512], F32, 4, scope=sbk)
                        SQB = sb("bSQB", [128, 2, 2, 512], F32, 4, scope=sbk)
                        MU = sb("bMU", [128, 2, 512], F32, 2, scope=sbk)
                        RB = sb("bRB", [128, 2, 512], F32, 2, scope=sbk)
                        T1 = sb("bT1", [128, 2, 512], F32, 2, scope=sbk)
                        SBt = sb("bSB", [128, 2, 2, 512], BF16, 4, scope=sbk)
                        for c in range(2):
                            if first:
                                memset("dve", HBF.v((A_, c, slice(0, 30)), c), 0.0)
                            else:
                                copy("dve", HBF.v((A_, c, slice(0, 30)), c), HBh[l].v((A_, c, A_)))
                        bst = {}

                        def Bdiag():
                            for c in range(2):
                                for j in range(31):
                                    dgv = V(DG.t[:, c * 31 + j, :], [DG.b[c]])
                                    cw = col(l, C_DWW + c * 31 + j)
                                    if j % 2 == 0:
                                        act(dgv, IDB.v(A_), AF.Copy, scale=cw)
                                    else:
                                        S.op("dve", lambda dgv=dgv, cw=cw: E["dve"].tensor_scalar_mul(out=dgv.ap, in0=IDB.t[:, :], scalar1=cw.ap),
                                             [IDB.v(A_), cw], [dgv])

                        def B1(t_):
                            for c in range(2):
                                bg = proj(WS1, 256 + c * 128, t_, 0)
                                act(SGB.v((A_, c, A_), c), bg.v(A_), AF.Sigmoid)
                                ba = proj(WS1, c * 128, t_, 0)
                                tt("dve", HBF.v((A_, c, slice(30 + t_ * NT, 30 + (t_ + 1) * NT)), c), ba.v(A_), SGB.v((A_, c, A_), c), ALU.mult)

                        def B2(t_):
                            for c in range(2):
                                cb = psum()
                                mm(cb.v(A_), [(V(DG.t[:, c * 31 + j, :], [DG.b[c]]), HBF.v((A_, c, slice(t_ * NT + j, t_ * NT + j + NT)), c)) for j in range(31)])
                                act(CB.v((A_, t_, c, A_), t_ * 2 + c), cb.v(A_), AF.Identity, bias=col(l, C_DWB + c))
                                act(SQB.v((A_, t_, c, A_), t_ * 2 + c), CB.v((A_, t_, c, A_), t_ * 2 + c), AF.Square)

                        def B3(t_):
                            mb = psum()
                            mm(mb.v(A_), [(ONEF.v(A_), CB.v((A_, t_, c, A_), t_ * 2 + c)) for c in range(2)])
                            qb = psum()
                            mm(qb.v(A_), [(ONEF.v(A_), SQB.v((A_, t_, c, A_), t_ * 2 + c)) for c in range(2)])
                            bst[t_] = (mb, qb)

                        def B4(t_):
                            mb, qb = bst[t_]
                            mu, rb, t1 = MU.v((A_, t_, A_), t_), RB.v((A_, t_, A_), t_), T1.v((A_, t_, A_), t_)
                            m2 = t1
                            ts("dve", mu, mb.v(A_), 1.0 / 256.0, None, ALU.mult)
                            tt("dve", m2, mu, mu, ALU.mult)
                            stt("dve", rb, qb.v(A_), 1.0 / 256.0, m2, ALU.mult, ALU.subtract)
                            act(rb, rb, AF.Ln, scale=1.0, bias=EPS_T)
                            act(rb, rb, AF.Exp, scale=-0.5)
                            for c in range(2):
                                tt("dve", t1, CB.v((A_, t_, c, A_), t_ * 2 + c), mu, ALU.subtract)
                                tt("dve", t1, t1, rb, ALU.mult)
                                act(SBt.v((A_, t_, c, A_), t_ * 2 + c), t1, AF.Silu, scale=col(l, C_LNG + c), bias=col(l, C_LNB + c))

                        def B5(t_):
                            cs = slice(t_ * NT, (t_ + 1) * NT)
                            for co in range(2):
                                pb = psum()
                                mm(pb.v(A_), [(WSM.v((A_, ci, slice(co * 128, (co + 1) * 128)), 0), SBt.v((A_, t_, ci, A_), t_ * 2 + ci)) for ci in range(2)])
                                act(YC.v((A_, 2 + co, cs), (2 + co) * 2 + t_), pb.v(A_), AF.Identity, bias=col(l, C_PWB + co))

                        B1(0)
                        B1(1)
                        Bdiag()
                        for fstage in (B2, B3, B4, B5):
                            for t_ in range(n_tiles):
                                fstage(t_)
                        for c in range(2):
                            copy("dve", HBh[l].v((A_, c, A_)), HBF.v((A_, c, slice(ST, ST + 30)), c))
                    if True:
                      if stage >= 5:
                        for c in range(2):
                            if first:
                                memset("dve", MCF.v((A_, c, slice(0, 2)), c), 0.0)
                            else:
                                copy("dve", MCF.v((A_, c, slice(0, 2)), c), MCh[l].v((A_, c, A_)))
                            for j in range(3):
                                act(V(DGC.t[:, c * 3 + j, :], DGC.b), IDB.v(A_), AF.Copy, scale=col(l, C_CCW + c * 3 + j))

                        def C1(t_):
                            for c in range(2):
                                bh = proj(WS0, 512 + c * 128, t_)
                                copy("act", HC.v((A_, c, A_), c), bh.v(A_))
                                bc_ = proj(WS0, 256 + c * 128, t_)
                                tt("dve", MCF.v((A_, c, slice(2 + t_ * NT, 2 + (t_ + 1) * NT)), c), bc_.v(A_), HC.v((A_, c, A_), c), ALU.mult)
                                bb = proj(WS0, c * 128, t_)
                                copy("act", BG.v((A_, t_, c, A_), t_ * 2 + c), bb.v(A_))

                        def C2(t_):
                            cs = slice(t_ * NT, (t_ + 1) * NT)
                            for c in range(2):
                                cb = psum()
                                mm(cb.v(A_), [(V(DGC.t[:, c * 3 + j, :], DGC.b), MCF.v((A_, c, slice(t_ * NT + j, t_ * NT + j + NT)), c)) for j in range(3)])
                                tt("dve", YC.v((A_, 4 + c, cs), (4 + c) * 2 + t_), cb.v(A_), BG.v((A_, t_, c, A_), t_ * 2 + c), ALU.mult)

                        for fstage in (C1, C2):
                            for t_ in range(n_tiles):
                                fstage(t_)
                        for c in range(2):
                            copy("dve", MCh[l].v((A_, c, A_)), MCF.v((A_, c, slice(ST, ST + 2)), c))
                    if True:
                      if stage >= 6:
                        for c in range(2):
                            if first:
                                memset("dve", UDF.v((A_, c, slice(0, 15)), c), 0.0)
                            else:
                                copy("dve", UDF.v((A_, c, slice(0, 15)), c), UDh[l].v((A_, c, A_)))

                        def D1(t_):
                            for c in range(2):
                                bu = proj(WS1, 512 + c * 128, t_, 1)
                                uf = UF.v((A_, t_, c, A_), t_ * 2 + c)
                                copy("act", uf, bu.v(A_))
                                copy("dve", UDF.v((A_, c, slice(15 + t_ * NT, 15 + (t_ + 1) * NT)), c), uf)

                        def D2(t_):
                            for c in range(2):
                                uf = UF.v((A_, t_, c, A_), t_ * 2 + c)
                                pbv = PB.v((A_, t_, c, A_), t_ * 2 + c)
                                ntap = 4 if c == 0 else 16
                                base = 0 if c == 0 else 4
                                wb = psum()
                                mm(wb.v(A_), [(V(DTAP.t[:, base + j, :], DTAP.b), UDF.v((A_, c, slice(15 + t_ * NT - j, 15 + t_ * NT - j + NT)), c)) for j in range(ntap)])
                                tt("dve", pbv, wb.v(A_), uf, ALU.subtract)
                                if first and t_ == 0:
                                    tt("dve", T16.v((A_, c, A_), c), wb.w(wb.t[:, 0:16]), DCORR.v((A_, c, A_)), ALU.mult)
                                    tt("dve", PB.v((A_, t_, c, slice(0, 16)), t_ * 2 + c), T16.v((A_, c, A_), c), UF.v((A_, t_, c, slice(0, 16)), t_ * 2 + c), ALU.subtract)

                        def D3(t_):
                            cs = slice(t_ * NT, (t_ + 1) * NT)
                            for c in range(2):
                                pb = psum()
                                mm(pb.v(A_), [(WSM.v((A_, 2 + c, slice(0, 128)), 1), PB.v((A_, t_, c, A_), t_ * 2 + c))])
                                act(YC.v((A_, 6 + c, cs), (6 + c) * 2 + t_), pb.v(A_), AF.Copy, scale=col(l, C_DSC + c))

                        for fstage in (D1, D2, D3):
                            for t_ in range(n_tiles):
                                fstage(t_)
                        for c in range(2):
                            copy("dve", UDh[l].v((A_, c, A_)), UDF.v((A_, c, slice(ST, ST + 15)), c))
                        S.barrier()
                    if debug and st == 0 and l == 0:
                        S.dma("sp", dbg_d[:, :, :], YC.t[:, :, :], sl_o, in_v=YC.va(A_))
                    S.dma("pool", WS0.t[:, :, 0:1024], w_out_d[l, :, :, :], sl_w0, out_v=WS0.v(A_))
                    with ExitStack() as so:
                      if stage >= 7:
                        YT = {"Y": sb("oY", [128, 8, 512], F32, 8, scope=so), "RS": sb("oRS", [128, 512], F32, 1, scope=so)}
                        SQ = sb("oSQ", [128, 4, 512], BF16, 4, scope=so)
                        for tt_ in range(n_tiles):
                            cs = slice(tt_ * NT, (tt_ + 1) * NT)
                            ssb = psum_hold()
                            pend = None
                            for m in range(8):
                                yb = psum()
                                mm(yb.v(A_), [(WS0.v((A_, k, slice(m * 128, (m + 1) * 128))), V(YC.t[:, k, cs], [YC.b[k * 2 + tt_]])) for k in range(8)])
                                if pend is not None:
                                    mm_acc(*pend)
                                copy("act", YT["Y"].v((A_, m, A_), m), yb.v(A_))
                                sq = SQ.v((A_, m % 4, A_), m % 4)
                                act(sq, yb.v(A_), AF.Square)
                                pend = (ssb.v(A_), ONEB.v(A_), sq, m == 0, m == 7)
                            mm_acc(*pend)
                            postnorm_residual(l, C_POST, YT, ssb, tt_)
                            psum_release(ssb)
                        S.barrier()
                S.barrier()
                if not do_ffn:
                    continue
                with ExitStack() as sf:
                    prenorm(l, C_FPRE, sf)
                    G = sb("fG", [128, 22, ST], BF16, 44, scope=sf)
                    UG = sb("fUG", [128, 2, 2, 2 + ST], BF16, 4, scope=sf)
                    DGF = sb("fDGF", [128, 2, 6, 128], BF16, 2, scope=sf)
                    SGT = sb("fSGT", [128, 2, 2, 512], F32, 4, scope=sf)
                    WU = [WS0, WS1]
                    wsl = [sl_w0, sl_w1]
                    def ffn_stage1(g, pi):
                        Wg = WU[g % 2]
                        jp = g * 2 + pi
                        r = jp % 2
                        for gu in range(2):
                            ch = jp + gu * 22
                            if gu == 1:
                                for j in range(3):
                                    act(V(DGF.t[:, r, j, :], [DGF.b[r]]), IDB.v(A_), AF.Copy, scale=col(l, C_FCW + ch * 3 + j))
                            uv = UG.v((A_, r, gu, slice(0, 2)), r * 2 + gu)
                            if first:
                                memset("dve", uv, 0.0)
                            else:
                                copy("dve", uv, V(FH[l].t[:, ch, :], FH[l].b))
                        for tt_ in range(n_tiles):
                            for gu in range(2):
                                pb = proj(Wg, gu * 256 + pi * 128, tt_)
                                uvw = UG.v((A_, r, gu, slice(2 + tt_ * NT, 2 + (tt_ + 1) * NT)), r * 2 + gu)
                                copy("act" if gu == 0 else "dve", uvw, pb.v(A_))
                            ct = SGT.v((A_, r, tt_, A_), r * 2 + tt_)
                            act(ct, UG.v((A_, r, 0, slice(tt_ * NT, tt_ * NT + NT)), r * 2), AF.Copy, scale=col(l, C_FCW + jp * 3 + 0))

                    def ffn_stage2(g, pi):
                        jp = g * 2 + pi
                        r = jp % 2
                        for tt_ in range(n_tiles):
                            ct = SGT.v((A_, r, tt_, A_), r * 2 + tt_)
                            for j in (1, 2):
                                stt("dve", ct, UG.v((A_, r, 0, slice(tt_ * NT + j, tt_ * NT + j + NT)), r * 2), col(l, C_FCW + jp * 3 + j), ct, ALU.mult, ALU.add)
                            cb = psum()
                            mm(cb.v(A_), [(V(DGF.t[:, r, j, :], [DGF.b[r]]),
                                           UG.v((A_, r, 1, slice(tt_ * NT + j, tt_ * NT + j + NT)), r * 2 + 1)) for j in range(3)])
                            act(ct, ct, AF.Silu)
                            tt("dve", G.v((A_, jp, slice(tt_ * NT, (tt_ + 1) * NT)), jp * 2 + tt_), cb.v(A_), ct, ALU.mult)
                        for gu in range(2):
                            ch = jp + gu * 22
                            copy("dve", V(FH[l].t[:, ch, :], FH[l].b), UG.v((A_, r, gu, slice(ST, ST + 2)), r * 2 + gu))

                    prev = None
                    for g in range(11):
                        Wg = WU[g % 2]
                        S.dma("pool", Wg.t[:, :, 0:256], w_up_d[l, :, :, g * 256:(g + 1) * 256], wsl[g % 2], out_v=Wg.va(A_))
                        S.dma("pool", Wg.t[:, :, 256:512], w_up_d[l, :, :, DFF + g * 256: DFF + (g + 1) * 256], wsl[g % 2], out_v=Wg.va(A_))
                        for pi in range(2):
                            ffn_stage1(g, pi)
                            if prev is not None:
                                ffn_stage2(*prev)
                            prev = (g, pi)
                    ffn_stage2(*prev)
                    YT = {"Y": sb("fY", [128, 2, 8, 512], F32, 16, scope=sf), "RS": sb("fRS", [128, 512], F32, 1, scope=sf)}
                    SQ = sb("fSQ", [128, 4, 512], BF16, 4, scope=sf)
                    ssbs = [psum_hold(), psum_hold()]
                    pend = None
                    it = 0
                    for mp in range(4):
                        Wg = WU[(mp + 1) % 2]
                        for hh in range(2):
                            S.dma("pool", Wg.t[:, :, hh * 256:(hh + 1) * 256],
                                  w_dn_d[l, :, hh * 8:(hh + 1) * 8, mp * 256:(mp + 1) * 256], wsl[(mp + 1) % 2], out_v=Wg.va(A_))
                        S.dma("pool", Wg.t[:, 0:6, 512:768], w_dn_d[l, :, 16:22, mp * 256:(mp + 1) * 256], wsl[(mp + 1) % 2], out_v=Wg.va(A_))
                        for mi in range(2):
                            m = mp * 2 + mi
                            for tt_ in range(n_tiles):
                                yb = psum()
                                pairs = []
                                for k in range(22):
                                    kk, part = k % 8, k // 8
                                    pairs.append((Wg.va((A_, kk, slice(part * 256 + mi * 128, part * 256 + (mi + 1) * 128))),
                                                  G.v((A_, k, slice(tt_ * NT, (tt_ + 1) * NT)), k * 2 + tt_)))
                                mm(yb.v(A_), pairs)
                                if pend is not None:
                                    mm_acc(*pend)
                                copy("act", V(YT["Y"].t[:, tt_, m, :], [YT["Y"].b[tt_ * 8 + m]]), yb.v(A_))
                                sq = SQ.v((A_, it % 4, A_), it % 4)
                                it += 1
                                act(sq, yb.v(A_), AF.Square)
                                pend = (ssbs[tt_].v(A_), ONEB.v(A_), sq, m == 0, m == 7)
                    mm_acc(*pend)
                    for tt_ in range(n_tiles):
                        cs = slice(tt_ * NT, (tt_ + 1) * NT)
                        RS = YT["RS"]
                        rstd_from_ss(RS.v(A_), ssbs[tt_].v(A_), float(D))
                        psum_release(ssbs[tt_])
                        for m in range(8):
                            y = V(YT["Y"].t[:, tt_, m, :], [YT["Y"].b[tt_ * 8 + m]])
                            tt("dve", y, y, RS.v(A_), ALU.mult)
                            stt("dve", X.v((A_, m, cs), m * 2 + tt_), y, col(l, C_FPOST + m), X.v((A_, m, cs), m * 2 + tt_), ALU.mult, ALU.add)
                    S.barrier()
            for xt_ in range(ST // NT):
                xs_ = slice(xt_ * NT, (xt_ + 1) * NT)
                S.dma("sp", out_d[seq, :, t0 + xt_ * NT:t0 + (xt_ + 1) * NT].rearrange("(k p) t -> p k t", p=128), X.t[:, :, xs_], sl_o,
                      in_v=V(X.t[:, :, xs_], [X.b[k * 2 + xt_] for k in range(8)]))
        E["sp"].wait_ge(sl_o.sem, sl_o.cnt)
        S.barrier(engines=("pe", "act", "dve", "pool", "sp"))
        print(f"[build] instructions={S.nins} waits={S.nwait} sems={S.nsem}")
    return nc


def _consts():
    ident = np.eye(128, dtype=np.float32)
    s = np.arange(64)[:, None]
    t = np.arange(64)[None, :]
    m = (s <= t).astype(np.float32)
    maskT = np.tile(m, (1, 8))
    p = np.arange(128)
    bmask = ((p[:, None] // 64) == (p[None, :] // 64)).astype(np.float32)
    rmask = np.ones((128, 512), np.float32)
    rmask[:, ::64] = 0.0
    wins = (2, 4, 8, 16)
    dtap = np.zeros((128, 20, 128), np.float32)
    dcorr = np.ones((128, 2, 16), np.float32)
    for c in range(2):
        base = 0 if c == 0 else 4
        ntap = 4 if c == 0 else 16
        for pp in range(128):
            w = wins[c * 2 + pp // 64]
            for j in range(ntap):
                if j < w:
                    dtap[pp, base + j, pp] = 1.0 / w
            for tt_ in range(16):
                dcorr[pp, c, tt_] = w / min(tt_ + 1.0, float(w))
    hm8 = np.zeros((128, 2), np.float32)
    hm8[:64, 0] = 0.125
    hm8[64:, 1] = 0.125
    return dict(c_ident=ident, c_maskT=maskT, c_bmask=bmask, c_rmask=rmask, c_dtap=dtap, c_dcorr=dcorr, c_hm8=hm8)


def _prep_weights(w_in, lb_gamma, a_norm_g, b_dw_w, b_dw_b, b_ln_g, b_ln_b, b_pw_w, b_pw_b,
                  c_conv_w, d_proj, d_scale, w_out, mix_pre_g, mix_post_g, ffn_pre_g, ffn_post_g,
                  w_up, ffn_conv_w, w_down):
    f = lambda a: np.ascontiguousarray(np.asarray(a, dtype=np.float32))
    L = DEPTH
    kp = lambda w, nk: f(np.asarray(w, np.float32).reshape(L, nk, 128, -1).transpose(0, 2, 1, 3))
    out = {}
    out["w_in_r"] = kp(w_in, 8)
    out["w_out_r"] = kp(w_out, 8)
    out["w_up_r"] = kp(w_up, 8)
    out["w_down_r"] = kp(w_down, 22)
    out["b_pw_r"] = kp(b_pw_w, 2)
    dp = np.zeros((L, 128, 2, 128), np.float32)
    dpj = np.asarray(d_proj, np.float32)
    for c in range(2):
        for gg in range(2):
            dp[:, gg * 64:(gg + 1) * 64, c, gg * 64:(gg + 1) * 64] = dpj[:, c * 2 + gg]
    out["d_proj_r"] = dp
    cols = np.zeros((L, 128, NCOL), np.float32)

    def put(base, vec, n):
        v = np.asarray(vec, np.float32).reshape(L, n, 128)
        for i in range(n):
            cols[:, :, base + i] = v[:, i]

    put(C_PRE, mix_pre_g, 8)
    put(C_POST, mix_post_g, 8)
    put(C_FPRE, ffn_pre_g, 8)
    put(C_FPOST, ffn_post_g, 8)
    put(C_ANG, a_norm_g, 2)
    put(C_DWB, b_dw_b, 2)
    put(C_LNG, b_ln_g, 2)
    put(C_LNB, b_ln_b, 2)
    put(C_PWB, b_pw_b, 2)
    put(C_DSC, d_scale, 2)
    put(C_LBG, lb_gamma, 2)
    dww = np.asarray(b_dw_w, np.float32).reshape(L, 31, 2, 128)
    for c in range(2):
        for j in range(31):
            cols[:, :, C_DWW + c * 31 + j] = dww[:, j, c]
    ccw = np.asarray(c_conv_w, np.float32).reshape(L, 3, 2, 128)
    for c in range(2):
        for j in range(3):
            cols[:, :, C_CCW + c * 3 + j] = ccw[:, j, c]
    fcw = np.asarray(ffn_conv_w, np.float32).reshape(L, 3, 44, 128)
    for ch in range(44):
        for j in range(3):
            cols[:, :, C_FCW + ch * 3 + j] = fcw[:, j, ch]
    out["cols"] = cols
    out.update(_consts())
    return out


_NC_CACHE = {}


def kernel(x, **weights):
    x = np.asarray(x, dtype=np.float32)
    shared = _prep_weights(**weights)
    if "nc" not in _NC_CACHE:
        _NC_CACHE["nc"] = build_program()
    nc = _NC_CACHE["nc"]
    in_maps = []
    for i in range(NCORES):
        xs = np.ascontiguousarray(x[2 * i:2 * i + 2].transpose(0, 2, 1))
        m = {"x_fm": xs}
        m.update(shared)
        in_maps.append(m)
    res = run_bass_kernel_spmd(nc, in_maps, core_ids=list(range(NCORES)))
    outs = [np.asarray(r["out_fm"], dtype=np.float32).transpose(0, 2, 1) for r in res.results]
    return np.ascontiguousarray(np.concatenate(outs, axis=0))
```
